# Optimizing a Trainium2 kernel written in Bass

```python
import math
import jax, jax.numpy as jnp
from jax import lax
import numpy as np

D_MODEL = 1024
BATCH = 4
SEQ = 4096
DEPTH = 1

CHUNK = 64
Q_BLOCK = 128
HEAD_DIM = 64
N_RET_HEADS = 8
N_FOX_HEADS = 8
RET_WIDTH = N_RET_HEADS * HEAD_DIM
FOX_WIDTH = N_FOX_HEADS * HEAD_DIM
MIX_WIDTH = RET_WIDTH + FOX_WIDTH
N_MEM = 256
N_XATTN_HEADS = 4
XATTN_HEAD_DIM = D_MODEL // N_XATTN_HEADS
D_FF = -(-8 * D_MODEL // (3 * 256)) * 256
ROPE_BASE = 10000.0
EPS = 1e-6
NEG_INF = -1e30
IN_SIZES = (RET_WIDTH, RET_WIDTH, RET_WIDTH, RET_WIDTH, FOX_WIDTH, FOX_WIDTH, FOX_WIDTH, N_FOX_HEADS)
IN_WIDTH = sum(IN_SIZES)

kernel_name = "hymba_retention_fox_hybrid_block"


def _split_points(sizes):
    pts, acc = [], 0
    for s in sizes[:-1]:
        acc += s
        pts.append(acc)
    return pts


def rmsnorm(x, g):
    xf = x.astype(jnp.float32)
    y = xf * lax.rsqrt(jnp.mean(xf * xf, axis=-1, keepdims=True) + EPS)
    return (y * g.astype(jnp.float32)).astype(x.dtype)


def head_group_norm(x, g):
    xf = x.astype(jnp.float32)
    mu = jnp.mean(xf, axis=-1, keepdims=True)
    xc = xf - mu
    var = jnp.mean(xc * xc, axis=-1, keepdims=True)
    return (xc * lax.rsqrt(var + EPS) * g.astype(jnp.float32)).astype(x.dtype)


def rotary(x, pos):
    d = x.shape[-1]
    inv_freq = ROPE_BASE ** (-jnp.arange(0, d, 2, dtype=jnp.float32) / d)
    ang = pos[:, None] * inv_freq[None, :]
    cos = jnp.cos(ang)[:, None, :].astype(x.dtype)
    sin = jnp.sin(ang)[:, None, :].astype(x.dtype)
    x1, x2 = x[..., : d // 2], x[..., d // 2:]
    return jnp.concatenate([x1 * cos - x2 * sin, x1 * sin + x2 * cos], axis=-1)


def chunk_retention(q, k, v):
    B, T, H, d = q.shape
    dv = v.shape[-1]
    nc = T // CHUNK
    dt = q.dtype
    log_g = jnp.log(1.0 - 2.0 ** (-5.0 - jnp.arange(H, dtype=jnp.float32)))
    idx = jnp.arange(CHUNK, dtype=jnp.float32)
    intra_decay = jnp.exp(log_g[:, None, None] * jnp.abs(idx[:, None] - idx[None, :])).astype(dt)
    q_decay = jnp.exp(log_g[None, :] * (idx[:, None] + 1.0)).astype(dt)
    k_decay = jnp.exp(log_g[None, :] * (CHUNK - 1.0 - idx[:, None])).astype(dt)
    chunk_decay = jnp.exp(log_g * CHUNK).astype(dt)[:, None, None]

    qc = (q * (d ** -0.5)).reshape(B, nc, CHUNK, H, d)
    kc = k.reshape(B, nc, CHUNK, H, d)
    vc = v.reshape(B, nc, CHUNK, H, dv)

    scores = jnp.einsum('bnihd,bnjhd->bnhij', qc, kc) * intra_decay
    intra = jnp.einsum('bnhij,bnjhe->bnihe', scores, vc)

    kv = jnp.einsum('bnjhd,bnjhe->nbhde', kc * k_decay[:, :, None], vc)

    def step(state, kv_c):
        return state * chunk_decay + kv_c, state

    _, s_prev = lax.scan(step, jnp.zeros((B, H, d, dv), kv.dtype), kv)
    inter = jnp.einsum('bnihd,nbhde->bnihe', qc * q_decay[:, :, None], s_prev)
    return (intra + inter).reshape(B, T, H, dv)


def forgetting_attention(q, k, v, log_f):
    B, T, H, d = q.shape
    F = jnp.cumsum(log_f, axis=1).transpose(0, 2, 1)
    qh = (q * (d ** -0.5)).transpose(0, 2, 1, 3)
    kh = k.transpose(0, 2, 1, 3)
    vh = v.transpose(0, 2, 1, 3)
    qpos = jnp.arange(Q_BLOCK)
    outs = []
    for blk in range(T // Q_BLOCK):
        q0 = blk * Q_BLOCK
        kend = q0 + Q_BLOCK
        logits = jnp.einsum('bhqd,bhkd->bhqk', qh[:, :, q0:kend], kh[:, :, :kend]).astype(jnp.float32)
        logits = logits + F[:, :, q0:kend, None] - F[:, :, None, :kend]
        causal = (q0 + qpos)[:, None] >= jnp.arange(kend)[None, :]
        logits = jnp.where(causal, logits, NEG_INF)
        p = jax.nn.softmax(logits, axis=-1).astype(v.dtype)
        outs.append(jnp.einsum('bhqk,bhkd->bhqd', p, vh[:, :, :kend]))
    return jnp.concatenate(outs, axis=2).transpose(0, 2, 1, 3)


def memory_cross_attention(hn, mem, w_xq, w_xkv, g_mem, g_xq, g_xk, w_xo):
    B, T, _ = hn.shape
    M = mem.shape[1]
    q = (hn @ w_xq).reshape(B, T, N_XATTN_HEADS, XATTN_HEAD_DIM)
    q = rmsnorm(q, g_xq)
    kv = rmsnorm(mem, g_mem) @ w_xkv
    k, v = jnp.split(kv, 2, axis=-1)
    k = rmsnorm(k.reshape(B, M, N_XATTN_HEADS, XATTN_HEAD_DIM), g_xk)
    v = v.reshape(B, M, N_XATTN_HEADS, XATTN_HEAD_DIM)
    logits = jnp.einsum('bthd,bmhd->bhtm', q, k).astype(jnp.float32) * (XATTN_HEAD_DIM ** -0.5)
    p = jax.nn.softmax(logits, axis=-1).astype(v.dtype)
    o = jnp.einsum('bhtm,bmhd->bthd', p, v).reshape(B, T, D_MODEL)
    return o @ w_xo


def setup_inputs(seed: int = 0) -> dict:
    key = jax.random.key(seed)
    ks = jax.random.split(key, 24)
    f32 = jnp.float32
    D = D_MODEL

    def w(k, shape, fan_in):
        return jax.random.normal(k, shape, f32) * fan_in ** -0.5

    def gain(k, shape):
        return 1.0 + 0.02 * jax.random.normal(k, shape, f32)

    return {
        "x": jax.random.normal(ks[0], (BATCH, SEQ, D), f32),
        "mem": jax.random.normal(ks[1], (BATCH, N_MEM, D), f32),
        "g_mix": gain(ks[2], (DEPTH, D)),
        "w_in": w(ks[3], (DEPTH, D, IN_WIDTH), D),
        "b_forget": jax.random.uniform(ks[4], (DEPTH, N_FOX_HEADS), f32, 1.0, 4.0),
        "g_ret_out": gain(ks[5], (DEPTH, N_RET_HEADS, HEAD_DIM)),
        "g_fox_q": gain(ks[6], (DEPTH, HEAD_DIM)),
        "g_fox_k": gain(ks[7], (DEPTH, HEAD_DIM)),
        "w_out": w(ks[8], (DEPTH, MIX_WIDTH, D), MIX_WIDTH),
        "g_xattn": gain(ks[9], (DEPTH, D)),
        "w_xq": w(ks[10], (DEPTH, D, D), D),
        "w_xkv": w(ks[11], (DEPTH, D, 2 * D), D),
        "g_mem": gain(ks[12], (DEPTH, D)),
        "g_xq": gain(ks[13], (DEPTH, XATTN_HEAD_DIM)),
        "g_xk": gain(ks[14], (DEPTH, XATTN_HEAD_DIM)),
        "w_xo": w(ks[15], (DEPTH, D, D), D),
        "g_ffn": gain(ks[16], (DEPTH, D)),
        "w_gate": w(ks[17], (DEPTH, D, D_FF), D),
        "w_up": w(ks[18], (DEPTH, D, D_FF), D),
        "w_down": w(ks[19], (DEPTH, D_FF, D), D_FF),
    }


def reference(x, mem, g_mix, w_in, b_forget, g_ret_out, g_fox_q, g_fox_k, w_out,
              g_xattn, w_xq, w_xkv, g_mem, g_xq, g_xk, w_xo,
              g_ffn, w_gate, w_up, w_down):
    B, T, _ = x.shape
    pos = jnp.arange(T, dtype=jnp.float32)
    splits = _split_points(IN_SIZES)
    h = x
    for l in range(DEPTH):
        hn = rmsnorm(h, g_mix[l])
        proj = hn @ w_in[l]
        rq, rk, rv, rg, fq, fk, fv, ff = jnp.split(proj, splits, axis=-1)

        rq = rotary(rq.reshape(B, T, N_RET_HEADS, HEAD_DIM), pos)
        rk = rotary(rk.reshape(B, T, N_RET_HEADS, HEAD_DIM), pos)
        ret = chunk_retention(rq, rk, rv.reshape(B, T, N_RET_HEADS, HEAD_DIM))
        ret = head_group_norm(ret, g_ret_out[l]).reshape(B, T, RET_WIDTH)
        ret = jax.nn.silu(rg) * ret

        fq = rmsnorm(fq.reshape(B, T, N_FOX_HEADS, HEAD_DIM), g_fox_q[l])
        fk = rmsnorm(fk.reshape(B, T, N_FOX_HEADS, HEAD_DIM), g_fox_k[l])
        log_f = jax.nn.log_sigmoid(ff.astype(jnp.float32) + b_forget[l].astype(jnp.float32))
        fox = forgetting_attention(fq, fk, fv.reshape(B, T, N_FOX_HEADS, HEAD_DIM), log_f)
        fox = fox.reshape(B, T, FOX_WIDTH)

        h = h + jnp.concatenate([ret, fox], axis=-1) @ w_out[l]

        h = h + memory_cross_attention(rmsnorm(h, g_xattn[l]), mem, w_xq[l], w_xkv[l],
                                       g_mem[l], g_xq[l], g_xk[l], w_xo[l])

        hn = rmsnorm(h, g_ffn[l])
        h = h + (jax.nn.silu(hn @ w_gate[l]) * (hn @ w_up[l])) @ w_down[l]
    return h
```

```python
import numpy as np
import concourse.bass as bass
import concourse.mybir as mybir
from concourse.bass_utils import run_bass_kernel_spmd

F32 = mybir.dt.float32
BF16 = mybir.dt.bfloat16
ALU = mybir.AluOpType
AF = mybir.ActivationFunctionType
AX = mybir.AxisListType

ENGS = ("tensor", "vector", "scalar", "gpsimd", "sync")
D = 1024
NT = 16
NTOK = 2048
DFF = 2816
EPS = 1e-6
NEG = -30000.0


class Op:
    __slots__ = ("eng", "fn", "deps", "sig", "is_dma", "dsem", "needed")

    def __init__(self, eng, fn, is_dma=False, dsem=None):
        self.eng = eng
        self.fn = fn
        self.deps = []
        self.sig = None
        self.is_dma = is_dma
        self.dsem = dsem
        self.needed = False


class Sched:
    def __init__(self, nc):
        self.nc = nc
        self.esem = {e: nc.alloc_semaphore("s_" + e) for e in ENGS}
        self.ecnt = {e: 0 for e in ENGS}
        self.dsems = {}
        self.dcnt = {}
        self.cur = {e: [] for e in ENGS}
        self.lastw = {}
        self.readers = {}
        self.seen = {e: {} for e in ENGS}
        self.bar = []
        self.bar_pending = {e: False for e in ENGS}

    def _add(self, o, reads, writes):
        deps = []
        if self.bar_pending[o.eng]:
            self.bar_pending[o.eng] = False
            o.deps.extend(self.bar)
        for k in reads:
            w = self.lastw.get(k)
            if w is not None:
                deps.append(w)
        for k in writes:
            w = self.lastw.get(k)
            if w is not None:
                deps.append(w)
            deps.extend(self.readers.get(k, ()))
        for k in reads:
            self.readers.setdefault(k, []).append(o)
        for k in writes:
            self.lastw[k] = o
            self.readers[k] = []
        seen = set()
        for d in deps:
            if d is o or id(d) in seen:
                continue
            seen.add(id(d))
            if d.eng == "tensor" and o.eng == "tensor" and not d.is_dma and not o.is_dma:
                continue
            o.deps.append(d)
            d.needed = True
        self.cur[o.eng].append(o)
        return o

    def op(self, eng, fn, reads=(), writes=()):
        return self._add(Op(eng, fn), reads, writes)

    def dma(self, eng, fn, sem, reads=(), writes=()):
        if sem not in self.dsems:
            self.dsems[sem] = self.nc.alloc_semaphore("d_" + sem)
            self.dcnt[sem] = 0
        o = Op(eng, fn, is_dma=True, dsem=sem)
        o.needed = True
        return self._add(o, reads, writes)

    def flush(self, final=False):
        nc = self.nc
        for e in ENGS:
            comp = [o for o in self.cur[e] if not o.is_dma]
            if comp:
                comp[-1].needed = True
            for o in self.cur[e]:
                if o.is_dma:
                    self.dcnt[o.dsem] += 16
                    o.sig = (self.dsems[o.dsem], self.dcnt[o.dsem])
                elif o.needed:
                    self.ecnt[e] += 1
                    o.sig = (self.esem[e], self.ecnt[e])
            nxt = None
            for o in reversed(self.cur[e]):
                if o.is_dma:
                    continue
                if o.sig is not None:
                    nxt = o.sig
                else:
                    o.sig = nxt
        bar = []
        for e in ENGS:
            comp = [o for o in self.cur[e] if not o.is_dma]
            if comp:
                bar.append(comp[-1])
        lastd = {}
        for e in ENGS:
            for o in self.cur[e]:
                if o.is_dma:
                    lastd[o.dsem] = o
        bar.extend(lastd.values())
        if bar:
            self.bar = bar
            self.bar_pending = {e: True for e in ENGS}
        cur = self.cur
        sched = self

        def emit(e, engine):
            seen = sched.seen[e]
            for o in cur[e]:
                for d in o.deps:
                    sem, val = d.sig
                    key = id(sem)
                    if seen.get(key, 0) >= val:
                        continue
                    seen[key] = val
                    engine.wait_ge(sem, val)
                ins = o.fn(engine)
                if o.is_dma:
                    ins.then_inc(o.sig[0], 16)
                elif o.needed:
                    ins.then_inc(o.sig[0], 1)
            if final and e == "sync":
                for name, sem in sched.dsems.items():
                    if sched.dcnt[name] > 0:
                        engine.wait_ge(sem, sched.dcnt[name])

        with nc.Block() as block:
            if cur["sync"] or final:
                block.sync(lambda eng: emit("sync", eng))
            if cur["tensor"]:
                block.tensor(lambda eng: emit("tensor", eng))
            if cur["vector"]:
                block.vector(lambda eng: emit("vector", eng))
            if cur["scalar"]:
                block.scalar(lambda eng: emit("scalar", eng))
            if cur["gpsimd"]:
                block.gpsimd(lambda eng: emit("gpsimd", eng))
        self.cur = {e: [] for e in ENGS}


def build(stage=99, debug=False):
    nc = bass.Bass("TRN2", target_bir_lowering=False)

    def din(name, shape):
        return nc.dram_tensor(name, list(shape), F32, kind="ExternalInput").ap()

    x_own = din("x_own", [NTOK, D])
    x_pre = din("x_pre", [NTOK, D])
    mem = din("mem", [256, D])
    w_in = din("w_in", [D, 3592])
    w_out = din("w_out", [D, D])
    w_xq = din("w_xq", [D, D])
    w_xkv = din("w_xkv", [D, 2 * D])
    w_xo = din("w_xo", [D, D])
    w_gate = din("w_gate", [D, DFF])
    w_up = din("w_up", [D, DFF])
    w_down = din("w_down", [DFF, D])
    gvec = din("gvec", [128, 32])
    rowc = din("rowc", [1, 1160])
    cst = din("cst", [128, 32 * 64])
    d2t = din("d2t", [128, 8 * 128])
    qdt = din("qdt", [128, 4 * 128])
    cdt = din("cdt", [128, 4 * 64])
    kdt = din("kdt", [128, 8])
    pbv = din("pbv", [128, 1])
    rt4 = din("rt4", [128, 4 * 512])
    y = nc.dram_tensor("y", [NTOK, D], F32, kind="ExternalOutput").ap()
    dbg = nc.dram_tensor("dbg", [128, 4096], F32, kind="ExternalOutput").ap() if debug else None

    S = Sched(nc)

    def sb(name, shape, dt=F32):
        return nc.alloc_sbuf_tensor(name, list(shape), dt)

    identf = sb("identf", [128, 128])
    ident = sb("ident", [128, 128], BF16)
    trif = sb("trif", [128, 128])
    onesf = sb("onesf", [128, 128])
    onesb = sb("onesb", [128, 128], BF16)
    tmb = sb("tmb", [128, 128], BF16)
    gv = sb("gv", [128, 32])
    rc = sb("rc", [128, 1160])
    gk = sb("gk", [128, 64])
    gxk = sb("gxk", [128, 256])
    d2 = sb("d2", [128, 8, 128])
    qd = sb("qd", [128, 4, 128])
    cd = sb("cd", [128, 4, 64])
    kd = sb("kd", [128, 8])
    pb = sb("pb", [128, 1])
    zf = sb("zf", [128, 32, 8])
    Gs = sb("Gs", [128, 32, 8])
    Gt = sb("Gt", [128, 32, 8])
    Sf = sb("Sf", [128, 4, 64])
    Sbf = sb("Sbf", [128, 4, 64], BF16)
    sm = sb("sm", [128, 64])
    ARENA_W = 47552
    arena = sb("arena", [128, ARENA_W])

    def V(off, shape, dt=F32, parts=128):
        n = 1
        for s_ in shape[1:]:
            n *= s_
        nbytes = n * (4 if dt == F32 else 2)
        assert off % 4 == 0 and nbytes % 4 == 0 and off + nbytes <= ARENA_W * 4, (off, shape)
        ap = arena[0:parts, off // 4:(off + nbytes) // 4]
        if dt != F32:
            ap = ap.bitcast(dt)
        if len(shape) == 3:
            ap = ap.rearrange("p (a b) -> p a b", a=shape[1])
        elif len(shape) == 4:
            ap = ap.rearrange("p (a b c) -> p a b c", a=shape[1], b=shape[2])
        return ap

    KB = 1024
    OX, OY = 0, 64 * KB
    OW = OY + 33280
    OM = OW + 42240
    OZ = OM + 16 * KB

    def ps(name, words=512):
        return nc.alloc_psum_tensor(name, [128, words], F32)

    P_sc = ps("P_sc", 1024)
    P_t = ps("P_t")
    P_a = ps("P_a")
    P_b = ps("P_b")
    P_c = ps("P_c")
    P_q = ps("P_q")
    P_o = ps("P_o")

    def pv(t, shape, dt=F32, parts=128):
        ap = t[0:parts, :]
        if dt != F32:
            ap = ap.bitcast(dt)
        n = 1
        for s_ in shape[1:]:
            n *= s_
        ap = ap[:, 0:n]
        if len(shape) == 3:
            ap = ap.rearrange("p (a b) -> p a b", a=shape[1])
        return ap

    def act(out, in_, func, reads, writes, **kw):
        return S.op("scalar", lambda e: e.activation(out=out, in_=in_, func=func, **kw), reads, writes)

    def tt(out, a, b, op, reads, writes, eng="vector"):
        return S.op(eng, lambda e: e.tensor_tensor(out=out, in0=a, in1=b, op=op), reads, writes)

    def ts(out, a, s1, s2, op0, op1, reads, writes, eng="vector"):
        if s2 is None:
            return S.op(eng, lambda e: e.tensor_scalar(out=out, in0=a, scalar1=s1, scalar2=None, op0=op0),
                        reads, writes)
        return S.op(eng, lambda e: e.tensor_scalar(out=out, in0=a, scalar1=s1, scalar2=s2, op0=op0, op1=op1),
                    reads, writes)

    def stt(out, a, sc, b, op0, op1, reads, writes, eng="vector"):
        return S.op(eng, lambda e: e.scalar_tensor_tensor(out=out, in0=a, scalar=sc, in1=b, op0=op0, op1=op1),
                    reads, writes)

    def cp(out, in_, reads, writes, eng="vector"):
        return S.op(eng, lambda e: e.tensor_copy(out=out, in_=in_), reads, writes)

    def red(out, in_, reads, writes):
        return S.op("vector", lambda e: e.tensor_reduce(out=out, in_=in_, axis=AX.X, op=ALU.add), reads, writes)

    def mm(out, lhsT, rhs, start, stop, reads, writes):
        return S.op("tensor", lambda e: e.matmul(out, lhsT=lhsT, rhs=rhs, start=start, stop=stop), reads, writes)

    def tr(out, in_, reads, writes):
        return S.op("tensor", lambda e: e.transpose(out=out, in_=in_, identity=ident[:]),
                    list(reads) + ["ident"], writes)

    def ld(out, in_, sem, writes, eng="sync", reads=()):
        return S.dma(eng, lambda e: e.dma_start(out=out, in_=in_), sem, reads=reads, writes=writes)

    def rstd_from(ss_ap, out_ap, n, reads, writes):
        act(out_ap, ss_ap, AF.Ln, reads, writes, scale=1.0 / n, bias=epsb[:])
        act(out_ap, out_ap, AF.Exp, writes, writes, scale=-0.5)

    epsb = sb("epsb", [128, 1])
    sel64 = sb("sel64", [128, 64])
    shiftb = sb("shiftb", [128, 128], BF16)
    oneb = sb("oneb", [128, 1])

    S.op("gpsimd", lambda e: e.memset(identf[:], 0.0), writes=["identf"])
    S.op("gpsimd", lambda e: e.affine_select(out=identf[:], in_=identf[:], pattern=[[-1, 128]],
                                             compare_op=ALU.not_equal, fill=1.0, base=0, channel_multiplier=1),
         reads=["identf"], writes=["identf"])
    cp(ident[:], identf[:], ["identf"], ["ident"])
    S.op("gpsimd", lambda e: e.memset(onesf[:], 1.0), writes=["onesf"])
    S.op("gpsimd", lambda e: e.memset(onesb[:], 1.0), writes=["onesb"])
    S.op("gpsimd", lambda e: e.memset(epsb[:], EPS), writes=["epsb"])
    S.op("gpsimd", lambda e: e.memset(oneb[:], 1.0), writes=["oneb"])
    S.op("gpsimd", lambda e: e.memset(sel64[:], 0.0), writes=["sel64"])
    S.op("gpsimd", lambda e: e.memset(sel64[64:65, :], 1.0), writes=["sel64"])
    S.op("gpsimd", lambda e: e.memset(shiftb[:], 0.0), writes=["shiftb"])
    cp(shiftb[0:64, 64:128], ident[0:64, 0:64], ["ident", "shiftb"], ["shiftb"])
    S.op("gpsimd", lambda e: e.affine_select(out=trif[:], in_=onesf[:], pattern=[[1, 128]],
                                             compare_op=ALU.is_ge, fill=0.0, base=0, channel_multiplier=-1),
         reads=["onesf"], writes=["trif"])
    S.op("gpsimd", lambda e: e.memset(identf[:], 0.0), reads=["ident"], writes=["identf"])
    S.op("gpsimd", lambda e: e.affine_select(out=identf[:], in_=identf[:], pattern=[[1, 128]],
                                             compare_op=ALU.is_ge, fill=NEG, base=0, channel_multiplier=-1),
         reads=["identf"], writes=["identf"])
    cp(tmb[:], identf[:], ["identf"], ["tmb"])
    ld(gv[:], gvec, "c0", ["gv"])
    ld(rc[:], rowc.partition_broadcast(128), "c1", ["rc"])
    ld(d2[:], d2t.rearrange("p (a b) -> p a b", a=8), "c2", ["d2"])
    ld(qd[:], qdt.rearrange("p (a b) -> p a b", a=4), "c3", ["qd"])
    ld(cd[:], cdt.rearrange("p (a b) -> p a b", a=4), "c4", ["cd"])
    ld(kd[:], kdt, "c5", ["kd"])
    ld(pb[:], pbv, "c6", ["pb"])
    stt(gk[:], rc[:, 512:576], 0.125, rc[:, 576:640], ALU.mult, ALU.mult, ["rc"], ["gk"])
    stt(gxk[:], rc[:, 648:904], 1.0 / 16, rc[:, 904:1160], ALU.mult, ALU.mult, ["rc"], ["gxk"])
    S.op("vector", lambda e: e.memset(Sf[:], 0.0), writes=["Sf"])
    S.op("vector", lambda e: e.memset(Sbf[:], 0.0), writes=["Sbf"])
    GM, GX, GME, GF = 0, 8, 16, 24

    def front(src, srck, gcol, hT, hTk, xb, xbk, rst, rstk, pre_scale, junk=None):
        front_a(src, srck, xb, xbk, rst, rstk, pre_scale, junk)
        front_b(gcol, hT, hTk)

    def front_b(gcol, hT, hTk):
        ptv = pv(P_t, [128, 8, 128], BF16)
        tt(hT, ptv, gv[:, gcol:gcol + 8].unsqueeze(2).to_broadcast([128, 8, 128]), ALU.mult,
           ["P_t", "gv"], [hTk])

    def front_a(src, srck, xb, xbk, rst, rstk, pre_scale, junk=None):
        if junk is None:
            act(xb, src, AF.Square, [srck], [xbk, "ssq"], accum_out=sm[:, 0:1])
        else:
            act(junk, src, AF.Square, [srck], ["junk", "ssq"], accum_out=sm[:, 0:1])
        rstd_from(sm[:, 0:1], rst, D, ["ssq", "epsb"], [rstk])
        if pre_scale:
            act(xb, src, AF.Copy, [srck, rstk], [xbk], scale=rst)
        else:
            cp(xb, src, [srck], [xbk])
        ptv = pv(P_t, [128, 8, 128], BF16)
        for c in range(8):
            tr(ptv[:, c, :], xb[:, c * 128:(c + 1) * 128], [xbk], ["P_t"])

    def proj(pst, psk, hT, hTk, w, wk, c0, n):
        for c in range(8):
            mm(pst[:, 0:n], hT[:, c, :], w[:, c, c0:c0 + n], c == 0, c == 7, [hTk, wk], [psk])

    wA = V(OX, [128, 8, 2048], BF16)
    mixT = V(OM, [128, 4, NTOK], BF16)
    csT = V(OZ + 24 * KB, [128, 32, 64])
    ld(csT, cst.rearrange("p (a b) -> p a b", a=32), "c7", ["csT"])
    for c in range(8):
        ld(wA[:, c, :], w_in[c * 128:(c + 1) * 128, 0:2048], "wA", ["wA%d" % c], eng="gpsimd")
    xts = [V(OZ + 0, [128, 1024]), V(OZ + 4 * KB, [128, 1024])]
    hTs = [V(OZ + 8 * KB, [128, 8, 128], BF16), V(OZ + 10 * KB, [128, 8, 128], BF16)]
    xbv = V(OZ + 12 * KB, [128, 1024], BF16)
    qkf = V(OZ + 14 * KB, [128, 2, 8, 64])
    rqkb = V(OZ + 18 * KB, [128, 2, 8, 64], BF16)
    rvb = V(OZ + 20 * KB, [128, 8, 64], BF16)
    kdb = V(OZ + 21 * KB, [128, 8, 64], BF16)
    QTr = V(OZ + 22 * KB, [128, 4, 128], BF16)
    QTd = V(OZ + 23 * KB, [128, 4, 128], BF16)
    KTr = V(OW + 0, [128, 4, 128], BF16)
    scm = V(OW + 1 * KB, [128, 8, 128], BF16)
    egt = V(OW + 3 * KB, [128, 512])
    sgt = V(OW + 5 * KB, [128, 512])
    tmpA = V(OW + 7 * KB, [128, 8, 64])
    tmpB = V(OW + 9 * KB, [128, 8, 64])
    retb = V(OW + 11 * KB, [128, 512], BF16)
    rot1 = V(OW + 12 * KB, [128, 16, 32])
    rot2 = V(OW + 14 * KB, [128, 16, 32])
    RT = V(OW + 16 * KB, [128, 4, 512])
    ld(RT, rt4.rearrange("p (a b) -> p a b", a=4), "c8", ["RT"])
    QT4 = [V(OW + 24 * KB + i * KB, [128, 4, 128], BF16) for i in range(4)]
    rstA = sb("rstA", [128, 2])
    nrst = sb("nrst", [128, 1])
    st8 = sb("st8", [128, 6, 8])

    def rotary(src, dst, nh, keys_r, keys_w, tile_idx):
        cosb = csT[:, tile_idx, 0:32].unsqueeze(1).to_broadcast([128, nh, 32])
        sinb = csT[:, tile_idx, 32:64].unsqueeze(1).to_broadcast([128, nh, 32])
        x1 = src[:, :, 0:32]
        x2 = src[:, :, 32:64]
        r1 = rot1[:, 0:nh, :]
        r2 = rot2[:, 0:nh, :]
        G = "gpsimd"
        tt(r1, x1, cosb, ALU.mult, keys_r + ["csT"], ["rot1"], eng=G)
        tt(r2, x2, sinb, ALU.mult, keys_r + ["csT"], ["rot2"], eng=G)
        tt(dst[:, :, 0:32], r1, r2, ALU.subtract, ["rot1", "rot2"], keys_w, eng=G)
        tt(r1, x1, sinb, ALU.mult, keys_r + ["csT"], ["rot1"], eng=G)
        tt(r2, x2, cosb, ALU.mult, keys_r + ["csT"], ["rot2"], eng=G)
        tt(dst[:, :, 32:64], r1, r2, ALU.add, ["rot1", "rot2"], keys_w, eng=G)

    import os
    qkfs = [qkf, V(OW + 28 * KB, [128, 2, 8, 64])]
    sgts = [sgt, V(OW + 32 * KB, [128, 512])]
    rvbs = [rvb, V(OW + 34 * KB, [128, 8, 64], BF16)]
    tilesA = list(range(2 * NT)) if stage >= 1 else []

    def xload(it):
        sl = it % 2
        xsrc = (x_own if it >= NT else x_pre)[(it % NT) * 128:(it % NT + 1) * 128, :]
        ld(xts[sl], xsrc, "xt%d" % sl, ["xt%d" % sl])

    junkA = V(OZ + 22 * KB, [128, 1024], BF16)
    junkB = V(OZ + 29 * KB, [128, 1024], BF16)
    cur_junk = [junkA]

    def FAa(it):
        sl = it % 2
        rst = rstA[:, sl:sl + 1]
        front_a(xts[sl], "xt%d" % sl, xbv, "xb", rst, "rst%d" % sl, False, junk=cur_junk[0])
        if it + 2 < 2 * NT:
            xload(it + 2)

    def FAb(it):
        sl = it % 2
        front_b(GM, hTs[sl], "hT%d" % sl)

    def FA(it):
        FAa(it)
        FAb(it)

    def PGA(it, g):
        own = it >= NT
        sl = it % 2
        rst = rstA[:, sl:sl + 1]
        hT, hTk = hTs[sl], "hT%d" % sl
        rk_ = ["rst%d" % sl]
        q_ = qkfs[sl]
        if g == "rq" and own:
            proj(P_a, "P_a", hT, hTk, wA, "wA7", 0, 512)
            act(q_[:, 0].rearrange("p a b -> p (a b)"), P_a[:, 0:512], AF.Copy, ["P_a"] + rk_, ["qkf0_%d" % sl], scale=rst)
        elif g == "rk":
            proj(P_b, "P_b", hT, hTk, wA, "wA7", 512, 512)
            act(q_[:, 1].rearrange("p a b -> p (a b)"), P_b[:, 0:512], AF.Copy, ["P_b"] + rk_, ["qkf1_%d" % sl], scale=rst)
        elif g == "rv":
            proj(P_c, "P_c", hT, hTk, wA, "wA7", 1024, 512)
            act(rvbs[sl].rearrange("p a b -> p (a b)"), P_c[:, 0:512], AF.Copy, ["P_c"] + rk_, ["rvb%d" % sl], scale=rst)
        elif g == "rg" and own:
            proj(P_a, "P_a", hT, hTk, wA, "wA7", 1536, 512)
            ts(nrst[:], rst, -1.0, None, ALU.mult, ALU.bypass, rk_, ["nrst"])
            act(egt, P_a[:, 0:512], AF.Exp, ["P_a", "nrst"], ["egt"], scale=nrst[:])
            act(egt, egt, AF.Ln, ["egt", "oneb"], ["egt"], bias=oneb[:])
            act(egt, egt, AF.Exp, ["egt"], ["egt"], scale=-1.0)

    def A1b(it):
        own = it >= NT
        sl = it % 2
        if own:
            rst = rstA[:, sl:sl + 1]
            stt(sgts[sl], P_a[:, 0:512], rst, egt, ALU.mult, ALU.mult, ["P_a", "egt", "rst%d" % sl], ["sgt%d" % sl])

    def ROT(it):
        own = it >= NT
        sl = it % 2
        q_ = qkfs[sl]
        if own:
            rotary(q_.rearrange("p a b c -> p (a b) c"), rqkb.rearrange("p a b c -> p (a b) c"), 16,
                   ["qkf0_%d" % sl, "qkf1_%d" % sl], ["rqkb"], it)
        else:
            rotary(q_[:, 1], rqkb[:, 1], 8, ["qkf1_%d" % sl], ["rqkb"], it)

    pqv = pv(P_q, [128, 8, 128], BF16)
    scv = pv(P_sc, [128, 8, 128])
    pov = pv(P_o, [128, 8, 64])
    pkv = pv(P_q, [128, 4, 128])

    def A2s1(it):
        own = it >= NT
        tt(kdb, rqkb[:, 1], kd[:].unsqueeze(2).to_broadcast([128, 8, 64]), ALU.mult, ["rqkb", "kd"], ["kdb"])
        if own:
            for p_ in range(4):
                tr(pqv[:, p_, :], rqkb[:, 0, 2 * p_:2 * p_ + 2, :].rearrange("p a b -> p (a b)"), ["rqkb"], ["P_q"])
            for p_ in range(4):
                tr(pqv[:, 4 + p_, :], rqkb[:, 1, 2 * p_:2 * p_ + 2, :].rearrange("p a b -> p (a b)"), ["rqkb"], ["P_q"])
            for i4 in range(4):
                tt(QT4[i4], pqv[:, 0:4, :], RT[:, i4, :].rearrange("p (a b) -> p a b", a=4), ALU.mult,
                   ["P_q", "RT"], ["QT4_%d" % i4])
            cp(KTr, pqv[:, 4:8, :], ["P_q"], ["KTr"])

    def A2s2(it):
        if it >= NT:
            for h in range(8):
                p_ = h // 2
                mm(scv[:, h, :], KTr[:, p_, :], QT4[h % 2][:, p_, :], True, True, ["KTr", "QT4_%d" % (h % 2)], ["P_sc"])
            tt(scm[:, 0:4, :], scv[:, 0:4, :], d2[:, 0:4, :], ALU.mult, ["P_sc", "d2"], ["scm"])
            tt(scm[:, 4:8, :], scv[:, 4:8, :], d2[:, 4:8, :], ALU.mult, ["P_sc", "d2"], ["scm"])

    def A2s3(it):
        own = it >= NT
        sl = it % 2
        rvb_ = rvbs[sl]
        rvk = "rvb%d" % sl
        if own:
            for h in range(8):
                p_ = h // 2
                mm(pov[:, h, :], scm[:, h, :], rvb_[:, h, :], True, False, ["scm", rvk], ["P_o"])
                mm(pov[:, h, :], QT4[2 + h % 2][:, p_, :], Sbf[:, p_, :], False, True, ["QT4_%d" % (2 + h % 2), "Sbf"], ["P_o"])
        for p_ in range(4):
            mm(pkv[:, p_, :], kdb[:, 2 * p_:2 * p_ + 2, :].rearrange("p a b -> p (a b)"),
               rvb_[:, 2 * p_:2 * p_ + 2, :].rearrange("p a b -> p (a b)"), True, True, ["kdb", rvk], ["P_q"])
        tt(Sf[:], Sf[:], cd[:], ALU.mult, ["Sf", "cd"], ["Sf"])
        tt(Sf[0:64], Sf[0:64], pkv[0:64, :, 0:64], ALU.add, ["Sf", "P_q"], ["Sf"])
        tt(Sf[64:128], Sf[64:128], pkv[64:128, :, 64:128], ALU.add, ["Sf", "P_q"], ["Sf"])
        cp(Sbf[:], Sf[:], ["Sf"], ["Sbf"])
        if own:
            s1, s2, mu, var, rg_, t0 = (st8[:, i, :] for i in range(6))
            red(s1, pov, ["P_o"], ["st_s1"])
            act(tmpA, pov, AF.Square, ["P_o"], ["tmpA"])
            red(s2, tmpA, ["tmpA"], ["st_s2"])
            ts(mu, s1, 1.0 / 64, None, ALU.mult, ALU.bypass, ["st_s1"], ["st_mu"])
            tt(t0, mu, mu, ALU.mult, ["st_mu"], ["st_t0"])
            stt(var, s2, 1.0 / 64, t0, ALU.mult, ALU.subtract, ["st_s2", "st_t0"], ["st_var"])
            act(rg_, var, AF.Ln, ["st_var", "epsb"], ["st_rg"], bias=epsb[:])
            act(rg_, rg_, AF.Exp, ["st_rg"], ["st_rg"], scale=-0.5)
            tt(tmpA, pov, mu.unsqueeze(2).to_broadcast([128, 8, 64]), ALU.subtract, ["P_o", "st_mu"], ["tmpA"])
            tt(tmpB, tmpA, rg_.unsqueeze(2).to_broadcast([128, 8, 64]), ALU.mult, ["tmpA", "st_rg"], ["tmpB"])

    def A2s4(it):
        if it >= NT:
            sl = it % 2
            tt(tmpA.rearrange("p a b -> p (a b)"), tmpB.rearrange("p a b -> p (a b)"), rc[:, 0:512], ALU.mult,
               ["tmpB", "rc"], ["tmpA"], eng="gpsimd")
            tt(retb, tmpA.rearrange("p a b -> p (a b)"), sgts[sl], ALU.mult, ["tmpA", "sgt%d" % sl], ["retb"], eng="gpsimd")

    def A2s5(it):
        if it >= NT:
            for c in range(4):
                tr(pqv[:, c, :], retb[:, c * 128:(c + 1) * 128], ["retb"], ["P_q"])
            t_ = it - NT
            cp(mixT[:, :, t_ * 128:(t_ + 1) * 128], pqv[:, 0:4, :], ["P_q"], ["mixT"])

    if tilesA:
        NA = 2 * NT
        xload(0)
        xload(1)
        FA(0)
        FA(1)
        for g in ("rq", "rk", "rv", "rg"):
            PGA(0, g)
        ROT(0)
        A1b(0)
        for k in range(NA):
            n = k + 1 if k + 1 < NA else None
            if k + 2 < NA:
                FA(k + 2)
            A2s1(k)
            if n is not None:
                PGA(n, "rq")
                PGA(n, "rk")
                ROT(n)
            A2s2(k)
            if k > 0:
                A2s5(k - 1)
            if n is not None:
                PGA(n, "rv")
            A2s3(k)
            if n is not None:
                PGA(n, "rg")
            A2s4(k)
            if n is not None:
                A1b(n)
        A2s5(NA - 1)
    S.flush()

    if debug and stage == 1:
        dbt = V(OW + 20 * KB, [128, 4096])
        cp(dbt[:, 0:2048], mixT[:, 0, :], ["mixT"], ["dbt"])
        cp(dbt[:, 2048:4096], mixT[:, 3, :], ["mixT"], ["dbt"])
        ld(dbg, dbt, "dbg", [], reads=["dbt"])

    KT = V(OX, [65, 8, 4096], BF16, parts=65)
    VA = V(OY, [128, 32, 8 * 65], BF16)
    wB = V(OW, [128, 8, 1544], BF16)
    Qn = V(OW + 25 * KB, [128, 16, 8 * 65], BF16)
    fqk = V(OZ + 14 * KB, [128, 2, 8, 64])
    sqf = V(OZ + 18 * KB, [128, 2, 8, 64])
    Kns = [V(OZ + 22 * KB, [128, 8, 65], BF16), V(OZ + 22 * KB + 1040, [128, 8, 65], BF16)]
    rh = sb("rh", [128, 16])
    if stage >= 2:
        for c in range(8):
            ld(wB[:, c, :], w_in[c * 128:(c + 1) * 128, 2048:3592], "wB", ["wB%d" % c], eng="gpsimd")
        S.op("gpsimd", lambda e: e.memset(VA, 1.0), writes=["VA"])
        for k_ in range(2):
            S.op("gpsimd", lambda e, k_=k_: e.memset(Kns[k_], 1.0), writes=["Kn%d" % k_])
    fqks = [fqk, V(OZ + 25 * KB, [128, 2, 8, 64])]

    def PGB(it, g):
        own = it >= NT
        sl = it % 2
        rst = rstA[:, sl:sl + 1]
        hT, hTk = hTs[sl], "hT%d" % sl
        rk_ = ["rst%d" % sl]
        f_ = fqks[sl]
        if g == "fq" and own:
            proj(P_a, "P_a", hT, hTk, wB, "wB7", 0, 512)
            act(f_[:, 0].rearrange("p a b -> p (a b)"), P_a[:, 0:512], AF.Copy, ["P_a"] + rk_, ["fqk0_%d" % sl], scale=rst)
        elif g == "fk":
            proj(P_b, "P_b", hT, hTk, wB, "wB7", 512, 512)
            act(f_[:, 1].rearrange("p a b -> p (a b)"), P_b[:, 0:512], AF.Copy, ["P_b"] + rk_, ["fqk1_%d" % sl], scale=rst)
        elif g == "fv":
            proj(P_c, "P_c", hT, hTk, wB, "wB7", 1024, 512)
            vav = VA[:, it, :].rearrange("p (a b) -> p a b", a=8)
            act(vav[:, :, 0:64], P_c[:, 0:512].rearrange("p (a b) -> p a b", a=8), AF.Copy, ["P_c"] + rk_, ["VA"], scale=rst)
        elif g == "ff":
            proj(P_o, "P_o", hT, hTk, wB, "wB7", 1536, 8)

    def B1b(it):
        sl = it % 2
        rst = rstA[:, sl:sl + 1]
        stt(zf[:, it, :], P_o[:, 0:8], rst, rc[:, 640:648], ALU.mult, ALU.add, ["P_o", "rc", "rst%d" % sl], ["zf"])

    def B2sq(it):
        own = it >= NT
        sl = it % 2
        f_ = fqks[sl]
        lo = 0 if own else 1
        src_ = f_.rearrange("p a b c -> p (a b) c")[:, lo * 8:16, :]
        sq_ = sqf.rearrange("p a b c -> p (a b) c")[:, lo * 8:16, :]
        rkeys = ["fqk0_%d" % sl, "fqk1_%d" % sl] if own else ["fqk1_%d" % sl]
        tt(sq_, src_, src_, ALU.mult, rkeys, ["sqf"], eng="gpsimd")

    def B2s1(it):
        own = it >= NT
        sl = it % 2
        f_ = fqks[sl]
        lo = 0 if own else 1
        src_ = f_.rearrange("p a b c -> p (a b) c")[:, lo * 8:16, :]
        sq_ = sqf.rearrange("p a b c -> p (a b) c")[:, lo * 8:16, :]
        red(rh[:, lo * 8:16], sq_, ["sqf"], ["rh"])
        act(rh[:, lo * 8:16], rh[:, lo * 8:16], AF.Ln, ["rh", "epsb"], ["rh"], scale=1.0 / 64, bias=epsb[:])
        act(rh[:, lo * 8:16], rh[:, lo * 8:16], AF.Exp, ["rh"], ["rh"], scale=-0.5)

    def B2s1b(it):
        own = it >= NT
        sl = it % 2
        f_ = fqks[sl]
        if own:
            qnv = Qn[:, it - NT, :].rearrange("p (a b) -> p a b", a=8)
            tt(qnv[:, :, 0:64], f_[:, 0], rh[:, 0:8].unsqueeze(2).to_broadcast([128, 8, 64]), ALU.mult,
               ["fqk0_%d" % sl, "rh"], ["Qn"])
        Kn, Knk = Kns[sl], "Kn%d" % sl
        tt(sqf[:, 1], f_[:, 1], rh[:, 8:16].unsqueeze(2).to_broadcast([128, 8, 64]), ALU.mult,
           ["fqk1_%d" % sl, "rh", "sqf"], ["sqf"])
        tt(Kn[:, :, 0:64], sqf[:, 1], gk[:].unsqueeze(1).to_broadcast([128, 8, 64]), ALU.mult, ["sqf", "gk"], [Knk])

    def B2s2(it):
        sl = it % 2
        Kn, Knk = Kns[sl], "Kn%d" % sl
        pkt = pv(P_q, [65, 8, 128], BF16, parts=65)
        for h in range(8):
            tr(pkt[:, h, :], Kn[:, h, :], [Knk], ["P_q"])
        cp(KT[:, :, it * 128:(it + 1) * 128], pkt, ["P_q"], ["KT"])

    if stage >= 2:
        NB = 2 * NT
        cur_junk[0] = junkB
        xload(0)
        xload(1)
        FA(0)
        FA(1)
        for g in ("fq", "fk", "fv", "ff"):
            PGB(0, g)
        B2sq(0)
        B1b(0)
        for k in range(NB):
            n = k + 1 if k + 1 < NB else None
            if k + 2 < NB:
                FAa(k + 2)
            B2s1(k)
            if k + 2 < NB:
                FAb(k + 2)
            B2s1b(k)
            if n is not None:
                PGB(n, "fq")
                PGB(n, "fk")
                B2sq(n)
                PGB(n, "fv")
            B2s2(k)
            if n is not None:
                PGB(n, "ff")
                B1b(n)
    if stage >= 2:
        zfl = zf[:].rearrange("p a b -> p (a b)")
        act(zfl, zfl, AF.Exp, ["zf"], ["zf"], scale=-1.0)
        act(zfl, zfl, AF.Ln, ["zf", "oneb"], ["zf"], bias=oneb[:])
        pg1 = P_a[:, 0:256]
        pg2 = P_b[:, 0:256]
        mm(pg1, trif[:], zfl, True, True, ["trif", "zf"], ["P_a"])
        mm(pg2, onesf[:], zfl, True, True, ["onesf", "zf"], ["P_b"])
        Gsl = Gs[:].rearrange("p a b -> p (a b)")
        cp(Gt[:].rearrange("p a b -> p (a b)"), pg2, ["P_b"], ["Gt"])
        cp(Gsl, pg1, ["P_a"], ["Gs"])
        for i in range(1, 32):
            tt(Gs[:, i, :], Gs[:, i, :], Gt[:, i - 1, :], ALU.add, ["Gs", "Gt"], ["Gs"])
            if i < 31:
                tt(Gt[:, i, :], Gt[:, i, :], Gt[:, i - 1, :], ALU.add, ["Gt"], ["Gt"])
        qn4 = Qn.rearrange("p a (b c) -> p a b c", b=8)
        ts(qn4[:, :, :, 64], Gs[:, 16:32, :], -1.0, None, ALU.mult, ALU.bypass, ["Gs"], ["Qn"])
        ts(Gs[:, 0:16, :], Gs[:, 0:16, :], pb[:], None, ALU.add, ALU.bypass, ["Gs", "pb"], ["Gs"])
    S.flush()

    QT = V(OZ, [65, 8, NTOK], BF16, parts=65)
    for t_ in range(NT if stage >= 2 else 0):
        pqt = pv(P_q if t_ % 2 == 0 else P_t, [65, 8, 128], BF16, parts=65)
        pk_ = "P_q" if t_ % 2 == 0 else "P_t"
        qnv = Qn[:, t_, :].rearrange("p (a b) -> p a b", a=8)
        for h in range(8):
            tr(pqt[:, h, :], qnv[:, h, :], ["Qn"], [pk_])
        cp(QT[:, :, t_ * 128:(t_ + 1) * 128], pqt, [pk_], ["QT"])
    S.flush()

    foxT2 = V(OW, [128, 4, NTOK], BF16)
    ftmp = V(OW + 16 * KB, [128, 512], BF16)
    pTs = [V(OW + 33 * KB + i * KB, [128, 512], BF16) for i in range(3)]
    osb = V(OW + 36 * KB, [128, 512])
    rl = V(OW + 38 * KB, [128, 512])
    wor = V(OW + 17 * KB, [128, 4, D], BF16)
    wof2 = V(OW + 25 * KB, [128, 4, D], BF16)
    if stage >= 4:
        for c in range(4):
            ld(wor[:, c, :], w_out[c * 128:(c + 1) * 128, :], "wor", ["wor%d" % c], eng="gpsimd")
        for p_ in range(4):
            ld(wof2[:, p_, :], w_out[512 + p_ * 128:512 + (p_ + 1) * 128, :], "wof", ["wof%d" % p_], eng="gpsimd")
    if stage >= 3:
        S.op("gpsimd", lambda e: e.memset(rl, 0.0), writes=["rl"])
    psS = [P_a, P_b, P_c]
    psk = ["P_a", "P_b", "P_c"]
    psO = [P_o, P_q]
    pok = ["P_o", "P_q"]
    tiles = []
    for h in range(8 if stage >= 3 else 0):
        for qi in range(4):
            nk = 16 + 4 * (qi + 1)
            for kt in range(nk):
                tiles.append((h, qi, kt, nk))

    def emit_S(idx):
        h, qi, kt, nk = tiles[idx]
        j = kt - (16 + 4 * qi)
        q0 = 128 * j if j > 0 else 0
        ps_, psk_ = psS[idx % 3], psk[idx % 3]
        diag = j >= 0
        mm(ps_[:, q0:512], KT[:, h, kt * 128:(kt + 1) * 128], QT[:, h, qi * 512 + q0:(qi + 1) * 512],
           True, not diag, ["KT", "QT"], [psk_])
        if diag:
            mm(ps_[:, q0:q0 + 128], ident[:], tmb[:], False, True, ["ident", "tmb"], [psk_])

    def make_norm(h, qi, po_, pk_o):
        def fn():
            S.op("vector", lambda e: e.reciprocal(out=rl[64:65, :], in_=po_[64:65, 0:512]), [pk_o], ["rl"])
            mm(P_t[0:64, 0:512], sel64[:], rl, True, True, ["sel64", "rl"], ["P_t"])
            cp(osb[0:64, :], po_[0:64, 0:512], [pk_o], ["osb"])
            cols = slice(qi * 512, (qi + 1) * 512)
            if h % 2 == 0:
                tt(foxT2[0:64, h // 2, cols], osb[0:64, :], P_t[0:64, 0:512], ALU.mult, ["osb", "P_t"], ["foxT"])
            else:
                tt(ftmp[0:64, :], osb[0:64, :], P_t[0:64, 0:512], ALU.mult, ["osb", "P_t"], ["ftmp"])
                mm(P_t[:, 0:512], shiftb[0:64, :], ftmp[0:64, :], True, True, ["shiftb", "ftmp"], ["P_t"])
                cp(foxT2[64:128, h // 2, cols], P_t[64:128, 0:512], ["P_t"], ["foxT"])
        return fn

    LOOK = 2
    for i0 in range(min(LOOK, len(tiles))):
        emit_S(i0)
    pending = None
    since = 0
    for idx, (h, qi, kt, nk) in enumerate(tiles):
        if idx + LOOK < len(tiles):
            emit_S(idx + LOOK)
        g = h * 4 + qi
        po_, pk_o = psO[g % 2], pok[g % 2]
        j = kt - (16 + 4 * qi)
        q0 = 128 * j if j > 0 else 0
        ps_, psk_ = psS[idx % 3], psk[idx % 3]
        pT, pTk = pTs[idx % 3], "pT%d" % (idx % 3)
        act(pT[:, q0:512], ps_[:, q0:512], AF.Exp, [psk_, "Gs"], [pTk], bias=Gs[:, kt, h:h + 1])
        vav = VA[:, kt, :].rearrange("p (a b) -> p a b", a=8)
        mm(po_[0:65, q0:512], vav[:, h, :], pT[:, q0:512], kt == 0, kt == nk - 1, ["VA", pTk], [pk_o])
        since += 1
        if pending is not None and since >= 6:
            pending()
            pending = None
        if kt == nk - 1:
            if pending is not None:
                pending()
            pending = make_norm(h, qi, po_, pk_o)
            since = 0
    if pending is not None:
        pending()
    S.flush()

    if debug and stage == 3:
        dbt = V(OZ, [128, 4096])
        S.op("vector", lambda e: e.memset(dbt, 0.0), ["QT"], ["dbt"])
        cp(dbt[0:64, 0:2048], foxT2[0:64, 0, :], ["foxT"], ["dbt"])
        cp(dbt[64:128, 2048:4096], foxT2[64:128, 3, :], ["foxT"], ["dbt"])
        ld(dbg, dbt, "dbg", [], reads=["dbt"])

    hres = V(OX, [128, NT, D])
    wq = V(OY, [128, 8, D], BF16)
    wo = V(OY + 16 * KB, [128, 8, D], BF16)
    if stage >= 5:
        for c in range(8):
            ld(wq[:, c, :], w_xq[c * 128:(c + 1) * 128, :], "wq", ["wq%d" % c], eng="gpsimd")
        for c in range(8):
            ld(wo[:, c, :], w_xo[c * 128:(c + 1) * 128, :], "wo", ["wo%d" % c], eng="gpsimd")
    x3 = [V(OZ + 0, [128, 1024]), V(OZ + 4 * KB, [128, 1024])]
    for t_ in range(NT if stage >= 4 else 0):
        sl = t_ % 2
        ld(x3[sl], x_own[t_ * 128:(t_ + 1) * 128, :], "xt%d" % sl, ["x3_%d" % sl])
        for half in range(2):
            ps_, psk_ = (P_a, "P_a") if half == 0 else (P_b, "P_b")
            for c in range(4):
                mm(ps_[:, 0:512], mixT[:, c, t_ * 128:(t_ + 1) * 128], wor[:, c, half * 512:(half + 1) * 512],
                   c == 0, False, ["mixT", "wor3"], [psk_])
            for p_ in range(4):
                mm(ps_[:, 0:512], foxT2[:, p_, t_ * 128:(t_ + 1) * 128], wof2[:, p_, half * 512:(half + 1) * 512],
                   False, p_ == 3, ["foxT", "wof3"], [psk_])
            tt(hres[:, t_, half * 512:(half + 1) * 512], x3[sl][:, half * 512:(half + 1) * 512], ps_[:, 0:512], ALU.add,
               ["x3_%d" % sl, psk_], ["h%d" % t_])
    S.flush()

    wq = V(OY, [128, 8, D], BF16)
    wo = V(OY + 16 * KB, [128, 8, D], BF16)
    wkv = V(OW, [128, 8, 2 * D], BF16)
    kTx = V(OM, [128, 8, 256], BF16)
    vbx = V(OM + 4 * KB, [128, 2, D], BF16)
    kfx = V(OM + 8 * KB, [128, D])
    ksq = V(OM + 12 * KB, [128, D])
    mt4 = [V(OZ + 0, [128, 1024]), V(OZ + 4 * KB, [128, 1024])]
    xb4 = V(OZ + 8 * KB, [128, 1024], BF16)
    hT4 = V(OZ + 10 * KB, [128, 8, 512], BF16)
    mT4 = V(OZ + 18 * KB, [128, 8, 256], BF16)
    kbx = V(OZ + 22 * KB, [128, D], BF16)
    qT4 = V(OW + 0, [128, 8, 512], BF16)
    sq4 = V(OW + 8 * KB, [128, 8, 512], BF16)
    rq4 = V(OW + 16 * KB, [128, 512])
    pT4 = [V(OW + 18 * KB, [128, 2, 512], BF16), V(OW + 20 * KB, [128, 2, 512], BF16)]
    oT4 = V(OW + 22 * KB, [128, 8, 512], BF16)
    rl4 = V(OW + 30 * KB, [128, 512])
    ou4 = V(OW + 32 * KB, [128, 512])
    rs4 = sb("rs4", [128, 8])
    if stage >= 5:
        for c in range(8):
            ld(wkv[:, c, :], w_xkv[c * 128:(c + 1) * 128, :], "wkv", ["wkv%d" % c], eng="gpsimd")
        for mtile in range(2):
            ld(mt4[mtile], mem[mtile * 128:(mtile + 1) * 128, :], "xt%d" % mtile, ["mt%d" % mtile])
            front(mt4[mtile], "mt%d" % mtile, GME, mT4[:, :, mtile * 128:(mtile + 1) * 128], "mT4", xb4, "xb4",
                  rs4[:, 0:1], "rs4a", True)
            for half in range(2):
                for c in range(8):
                    mm(P_a[:, 0:512], mT4[:, c, mtile * 128:(mtile + 1) * 128], wkv[:, c, half * 512:(half + 1) * 512],
                       c == 0, c == 7, ["mT4", "wkv7"], ["P_a"])
                act(kfx[:, half * 512:(half + 1) * 512], P_a[:, 0:512], AF.Copy, ["P_a"], ["kfx"])
            for half in range(2):
                for c in range(8):
                    mm(P_b[:, 0:512], mT4[:, c, mtile * 128:(mtile + 1) * 128],
                       wkv[:, c, D + half * 512:D + (half + 1) * 512], c == 0, c == 7, ["mT4", "wkv7"], ["P_b"])
                act(vbx[:, mtile, half * 512:(half + 1) * 512], P_b[:, 0:512], AF.Copy, ["P_b"], ["vbx"])
            tt(ksq, kfx, kfx, ALU.mult, ["kfx"], ["ksq"])
            red(rs4[:, 4:8], ksq.rearrange("p (a b) -> p a b", a=4), ["ksq"], ["rs4k"])
            act(rs4[:, 4:8], rs4[:, 4:8], AF.Ln, ["rs4k", "epsb"], ["rs4k"], scale=1.0 / 256, bias=epsb[:])
            act(rs4[:, 4:8], rs4[:, 4:8], AF.Exp, ["rs4k"], ["rs4k"], scale=-0.5)
            tt(ksq.rearrange("p (a b) -> p a b", a=4), kfx.rearrange("p (a b) -> p a b", a=4),
               rs4[:, 4:8].unsqueeze(2).to_broadcast([128, 4, 256]), ALU.mult, ["kfx", "rs4k", "ksq"], ["ksq"])
            tt(kbx.rearrange("p (a b) -> p a b", a=4), ksq.rearrange("p (a b) -> p a b", a=4),
               gxk[:].unsqueeze(1).to_broadcast([128, 4, 256]), ALU.mult, ["ksq", "gxk"], ["kbx"])
            pkx = pv(P_q, [128, 8, 128], BF16)
            for c in range(8):
                tr(pkx[:, c, :], kbx[:, c * 128:(c + 1) * 128], ["kbx"], ["P_q"])
            cp(kTx[:, :, mtile * 128:(mtile + 1) * 128], pkx, ["P_q"], ["kTx"])
    S.flush()
    hT4s = [V(OZ + 0, [128, 8, 512], BF16), V(OZ + 10 * KB, [128, 8, 512], BF16)]
    xb4s = [V(OZ + 8 * KB, [128, 1024], BF16), V(OZ + 18 * KB, [128, 1024], BF16)]
    junk4 = V(OZ + 20 * KB, [128, 1024], BF16)
    rq4s = [V(OZ + 24 * KB + i * 2 * KB, [128, 512]) for i in range(4)]
    pT4s = [V(OW + 16 * KB + i * 2 * KB, [128, 2, 512], BF16) for i in range(4)]
    oT4 = V(OW + 24 * KB, [128, 8, 512], BF16)
    rl4s = [V(OW + 32 * KB + i * 2 * KB, [128, 512]) for i in range(2)]

    def F4(b):
        for ti in range(4):
            t_ = b * 4 + ti
            k_ = ti % 2
            front(hres[:, t_, :], "h%d" % t_, GX, hT4s[b % 2][:, :, ti * 128:(ti + 1) * 128], "hT4_%d" % (b % 2),
                  xb4s[k_], "xb4_%d" % k_, rs4[:, 1 + k_:2 + k_], "rs4b%d" % k_, True, junk=junk4)

    def Q4(b):
        hT, hk = hT4s[b % 2], "hT4_%d" % (b % 2)
        for c2 in range(8):
            ps_, psk_ = (P_a, "P_a") if c2 % 2 == 0 else (P_b, "P_b")
            for c in range(8):
                mm(ps_[:, 0:512], wq[:, c, c2 * 128:(c2 + 1) * 128], hT[:, c, :], c == 0, c == 7, ["wq7", hk], [psk_])
            cp(qT4[:, c2, :], ps_[:, 0:512], [psk_], ["qT4_%d" % c2])
            tt(sq4[:, c2, :], qT4[:, c2, :], qT4[:, c2, :], ALU.mult, ["qT4_%d" % c2], ["sq4_%d" % c2], eng="gpsimd")

    def H4(b):
        for hd in range(4):
            mm(P_c[:, 0:512], onesb[:], sq4[:, 2 * hd, :], True, False, ["onesb", "sq4_%d" % (2 * hd)], ["P_c"])
            mm(P_c[:, 0:512], onesb[:], sq4[:, 2 * hd + 1, :], False, True, ["onesb", "sq4_%d" % (2 * hd + 1)], ["P_c"])
            rk = "rq4_%d" % hd
            act(rq4s[hd], P_c[:, 0:512], AF.Ln, ["P_c", "epsb"], [rk], scale=1.0 / 256, bias=epsb[:])
            act(rq4s[hd], rq4s[hd], AF.Exp, [rk], [rk], scale=-0.5)
            for half in range(2):
                c2 = 2 * hd + half
                tt(qT4[:, c2, :], qT4[:, c2, :], rq4s[hd], ALU.mult, ["qT4_%d" % c2, rk], ["qT4_%d" % c2])
        for hd in range(4):
            pT_, pTk = pT4s[hd], "pT4_%d" % hd
            for mtile in range(2):
                ps_, psk_ = (P_a, "P_a") if mtile == 0 else (P_b, "P_b")
                for half in range(2):
                    c2 = 2 * hd + half
                    mm(ps_[:, 0:512], kTx[:, c2, mtile * 128:(mtile + 1) * 128], qT4[:, c2, :], half == 0, half == 1,
                       ["kTx", "qT4_%d" % c2], [psk_])
                act(pT_[:, mtile, :], ps_[:, 0:512], AF.Exp, [psk_], [pTk])
        for hd in range(4):
            pT_, pTk = pT4s[hd], "pT4_%d" % hd
            rl_, rlk = rl4s[hd % 2], "rl4_%d" % (hd % 2)
            mm(P_c[:, 0:512], onesb[:], pT_[:, 0, :], True, False, ["onesb", pTk], ["P_c"])
            mm(P_c[:, 0:512], onesb[:], pT_[:, 1, :], False, True, ["onesb", pTk], ["P_c"])
            act(rl_, P_c[:, 0:512], AF.Ln, ["P_c"], [rlk])
            act(rl_, rl_, AF.Exp, [rlk], [rlk], scale=-1.0)
            for half in range(2):
                c2 = 2 * hd + half
                po_, pk_o = (P_o, "P_o") if half == 0 else (P_q, "P_q")
                for mtile in range(2):
                    mm(po_[:, 0:512], vbx[:, mtile, c2 * 128:(c2 + 1) * 128], pT_[:, mtile, :], mtile == 0, mtile == 1,
                       ["vbx", pTk], [pk_o])
                tt(oT4[:, c2, :], po_[:, 0:512], rl_, ALU.mult, [pk_o, rlk], ["oT4"])

    def O4(b):
        for ti in range(4):
            t_ = b * 4 + ti
            for half in range(2):
                ps_, psk_ = (P_a, "P_a") if half == 0 else (P_b, "P_b")
                for c in range(8):
                    mm(ps_[:, 0:512], oT4[:, c, ti * 128:(ti + 1) * 128], wo[:, c, half * 512:(half + 1) * 512],
                       c == 0, c == 7, ["oT4", "wo7"], [psk_])
                tt(hres[:, t_, half * 512:(half + 1) * 512], hres[:, t_, half * 512:(half + 1) * 512], ps_[:, 0:512],
                   ALU.add, ["h%d" % t_, psk_], ["h%d" % t_])

    if stage >= 5:
        F4(0)
        Q4(0)
        for b_ in range(4):
            if b_ + 1 < 4:
                F4(b_ + 1)
            H4(b_)
            if b_ + 1 < 4:
                Q4(b_ + 1)
            O4(b_)
    S.flush()

    hn5 = V(OZ, [128, 8, NTOK], BF16)
    R0 = OY
    PARTS = [6, 6, 5, 5]

    def wslot(s_):
        base = R0 + s_ * 36 * KB
        return (V(base, [128, 8, 768], BF16), V(base + 12 * KB, [128, 8, 768], BF16),
                V(base + 24 * KB, [128, 6, D], BF16))

    aT = V(R0 + 72 * KB, [128, 6, 512], BF16)
    sg5 = V(R0 + 78 * KB, [128, 512])
    xb5 = V(R0 + 80 * KB, [128, 1024], BF16)
    rs5 = sb("rs5", [128, 1])
    xb5s = [xb5, V(R0 + 82 * KB, [128, 1024], BF16)]
    junk5 = V(R0 + 84 * KB, [128, 1024], BF16)
    rs5b = sb("rs5b", [128, 2])

    last_key = {}

    def load_part(pi):
        nf = PARTS[pi]
        f0 = sum(PARTS[:pi])
        wg_, wu_, wd_ = wslot(pi % 2)
        sk = "ws%d" % (pi % 2)
        i_ = [0]

        def one(dst, src):
            keys = ["%s_%d_%d" % (sk, pi, i_[0])]
            if i_[0] == 0 and pi >= 2:
                keys.append(last_key[pi - 2])
            ld(dst, src, sk, keys, eng="gpsimd")
            i_[0] += 1
            return keys[0]
        last = None
        for c in range(8):
            one(wg_[:, c, 0:nf * 128], w_gate[c * 128:(c + 1) * 128, f0 * 128:(f0 + nf) * 128])
            one(wu_[:, c, 0:nf * 128], w_up[c * 128:(c + 1) * 128, f0 * 128:(f0 + nf) * 128])
        for f in range(nf):
            last = one(wd_[:, f, :], w_down[(f0 + f) * 128:(f0 + f + 1) * 128, :])
        last_key[pi] = last
        return [last]

    def ffn_fronts(blk):
        for t_ in range(blk * 4, blk * 4 + 4):
            k_ = t_ % 2
            front(hres[:, t_, :], "h%d" % t_, GF, hn5[:, :, t_ * 128:(t_ + 1) * 128], "hn5_%d" % (t_ // 4),
                  xb5s[k_], "xb5_%d" % k_, rs5b[:, k_:k_ + 1], "rs5_%d" % k_, True, junk=junk5)

    def ffn_part(pi, wkeys):
        nf = PARTS[pi]
        wg_, wu_, wd_ = wslot(pi % 2)
        for blk in range(4):
            if pi == 0 and blk + 1 < 4:
                ffn_fronts(blk + 1)
            hk = "hn5_%d" % blk
            for f in range(nf):
                for c in range(8):
                    mm(P_a[:, 0:512], wg_[:, c, f * 128:(f + 1) * 128], hn5[:, c, blk * 512:(blk + 1) * 512],
                       c == 0, c == 7, wkeys + [hk], ["P_a"])
                for c in range(8):
                    mm(P_b[:, 0:512], wu_[:, c, f * 128:(f + 1) * 128], hn5[:, c, blk * 512:(blk + 1) * 512],
                       c == 0, c == 7, wkeys + [hk], ["P_b"])
                act(sg5, P_a[:, 0:512], AF.Silu, ["P_a"], ["sg5"])
                tt(aT[:, f, :], sg5, P_b[:, 0:512], ALU.mult, ["sg5", "P_b"], ["aT%d" % f])
            for ti in range(4):
                t_ = blk * 4 + ti
                for half in range(2):
                    ps_, psk_ = (P_c, "P_c") if half == 0 else (P_o, "P_o")
                    for f in range(nf):
                        mm(ps_[:, 0:512], aT[:, f, ti * 128:(ti + 1) * 128], wd_[:, f, half * 512:(half + 1) * 512],
                           f == 0, f == nf - 1, ["aT%d" % f] + wkeys, [psk_])
                    tt(hres[:, t_, half * 512:(half + 1) * 512], hres[:, t_, half * 512:(half + 1) * 512],
                       ps_[:, 0:512], ALU.add, ["h%d" % t_, psk_], ["h%d" % t_])

    if stage >= 6:
        wk0 = load_part(0)
        wk1 = load_part(1)
        ffn_fronts(0)
        ffn_part(0, wk0)
        wk2 = load_part(2)
        ffn_part(1, wk1)
        wk3 = load_part(3)
        ffn_part(2, wk2)
        ffn_part(3, wk3)
    if stage >= 4:
        for t_ in range(NT):
            ld(y[t_ * 128:(t_ + 1) * 128, :], hres[:, t_, :], "yout", [], reads=["h%d" % t_])
    S.flush(final=True)
    return nc


def _consts(s):
    H = 8
    idx = np.arange(128, dtype=np.float32)
    log_g = np.log(1.0 - 2.0 ** (-5.0 - np.arange(H, dtype=np.float32))).astype(np.float32)
    inv_freq = (10000.0 ** (-np.arange(0, 64, 2, dtype=np.float32) / 64)).astype(np.float32)
    pos_own = np.arange(NTOK, dtype=np.float32) + s * NTOK
    pos_pre = np.arange(NTOK, dtype=np.float32)
    pos = np.concatenate([pos_pre, pos_own]).astype(np.float32)
    ang = (pos[:, None] * inv_freq[None, :]).astype(np.float32)
    cs = np.concatenate([np.cos(ang), np.sin(ang)], axis=1).astype(np.float32)
    cst = cs.reshape(32, 128, 64).transpose(1, 0, 2).reshape(128, 32 * 64)
    i_ = idx[None, :]
    j_ = idx[:, None]
    same = (i_ // 64) == (j_ // 64)
    lower = (i_ >= 64) & (j_ < 64)
    d2 = np.zeros((128, H, 128), np.float32)
    for h in range(H):
        a = np.exp(log_g[h] * np.abs(i_ - j_)).astype(np.float32)
        b = np.exp(log_g[h] * (i_ - j_)).astype(np.float32)
        d2[:, h, :] = np.where(same, a, np.where(lower, b, 0.0)) * 0.125
    qd = np.zeros((128, 4, 128), np.float32)
    cd = np.zeros((128, 4, 64), np.float32)
    for p in range(128):
        for pr in range(4):
            h = 2 * pr + p // 64
            qd[p, pr, :] = np.exp(log_g[h] * (idx + 1.0)) * 0.125
            cd[p, pr, :] = np.exp(log_g[h] * 128.0)
    kd = np.exp(log_g[None, :] * (127.0 - idx[:, None])).astype(np.float32)
    me = np.zeros((128, 4, 128), np.float32); me[0:64] = 1.0
    mo = np.zeros((128, 4, 128), np.float32); mo[64:128] = 1.0
    rt4 = np.concatenate([me.reshape(128, -1), mo.reshape(128, -1), (qd * me).reshape(128, -1),
                          (qd * mo).reshape(128, -1)], axis=1).astype(np.float32)
    return dict(rt4=np.ascontiguousarray(rt4), cst=np.ascontiguousarray(cst), d2t=d2.reshape(128, -1), qdt=qd.reshape(128, -1),
                cdt=cd.reshape(128, -1), kdt=kd)


_NC_CACHE = {}


def kernel(x, mem, g_mix, w_in, b_forget, g_ret_out, g_fox_q, g_fox_k, w_out, g_xattn, w_xq, w_xkv, g_mem,
           g_xq, g_xk, w_xo, g_ffn, w_gate, w_up, w_down, _stage=99, _debug=False):
    f = lambda a: np.ascontiguousarray(np.asarray(a, dtype=np.float32))
    x, mem = f(x), f(mem)
    key = (_stage, _debug)
    if key not in _NC_CACHE:
        _NC_CACHE[key] = build(_stage, _debug)
    nc = _NC_CACHE[key]
    gcol = lambda g: f(g).reshape(8, 128).T
    gvec = np.ascontiguousarray(np.concatenate([gcol(g_mix[0]), gcol(g_xattn[0]), gcol(g_mem[0]), gcol(g_ffn[0])], axis=1))
    rowc = np.concatenate([f(g_ret_out[0]).reshape(-1), f(g_fox_q[0]), f(g_fox_k[0]), f(b_forget[0]),
                           f(g_xq[0]), f(g_xk[0])])[None, :]
    shared = dict(w_in=f(w_in[0]), w_out=f(w_out[0]), w_xq=f(w_xq[0]), w_xkv=f(w_xkv[0]), w_xo=f(w_xo[0]),
                  w_gate=f(w_gate[0]), w_up=f(w_up[0]), w_down=f(w_down[0]), gvec=gvec, rowc=f(rowc))
    cs = [_consts(0), _consts(1)]
    zeros = np.zeros((NTOK, D), np.float32)
    in_maps = []
    for core in range(8):
        b, s = core // 2, core % 2
        m = dict(shared)
        m.update(cs[s])
        m["x_own"] = np.ascontiguousarray(x[b, s * NTOK:(s + 1) * NTOK])
        m["x_pre"] = np.ascontiguousarray(x[b, 0:NTOK]) if s == 1 else zeros
        m["mem"] = mem[b]
        m["pbv"] = np.full((128, 1), 0.0 if s == 1 else NEG, np.float32)
        in_maps.append(m)
    res = run_bass_kernel_spmd(nc, in_maps, core_ids=list(range(8)))
    out = np.empty((4, 4096, D), np.float32)
    for core in range(8):
        b, s = core // 2, core % 2
        out[b, s * NTOK:(s + 1) * NTOK] = res.results[core]["y"]
    if _debug:
        return out, [r["dbg"] for r in res.results]
    return out
```

```python
import numpy as np
import concourse.bass as bass
import concourse.mybir as mybir
from concourse.bass_utils import run_bass_kernel_spmd

F32 = mybir.dt.float32
BF16 = mybir.dt.bfloat16
ALU = mybir.AluOpType
AF = mybir.ActivationFunctionType
AX = mybir.AxisListType

ENGS = ("tensor", "vector", "scalar", "gpsimd", "sync")
D = 1024
NT = 16
NTOK = 2048
DFF = 2816
EPS = 1e-6
NEG = -30000.0


class Op:
    __slots__ = ("eng", "fn", "deps", "sig", "is_dma", "dsem", "needed")

    def __init__(self, eng, fn, is_dma=False, dsem=None):
        self.eng = eng
        self.fn = fn
        self.deps = []
        self.sig = None
        self.is_dma = is_dma
        self.dsem = dsem
        self.needed = False


class Sched:
    def __init__(self, nc):
        self.nc = nc
        self.esem = {e: nc.alloc_semaphore("s_" + e) for e in ENGS}
        self.ecnt = {e: 0 for e in ENGS}
        self.dsems = {}
        self.dcnt = {}
        self.cur = {e: [] for e in ENGS}
        self.lastw = {}
        self.readers = {}
        self.seen = {e: {} for e in ENGS}
        self.bar = []
        self.bar_pending = {e: False for e in ENGS}

    def _add(self, o, reads, writes):
        deps = []
        if self.bar_pending[o.eng]:
            self.bar_pending[o.eng] = False
            o.deps.extend(self.bar)
        for k in reads:
            w = self.lastw.get(k)
            if w is not None:
                deps.append(w)
        for k in writes:
            w = self.lastw.get(k)
            if w is not None:
                deps.append(w)
            deps.extend(self.readers.get(k, ()))
        for k in reads:
            self.readers.setdefault(k, []).append(o)
        for k in writes:
            self.lastw[k] = o
            self.readers[k] = []
        seen = set()
        for d in deps:
            if d is o or id(d) in seen:
                continue
            seen.add(id(d))
            if d.eng == "tensor" and o.eng == "tensor" and not d.is_dma and not o.is_dma:
                continue
            o.deps.append(d)
            d.needed = True
        self.cur[o.eng].append(o)
        return o

    def op(self, eng, fn, reads=(), writes=()):
        return self._add(Op(eng, fn), reads, writes)

    def dma(self, eng, fn, sem, reads=(), writes=()):
        if sem not in self.dsems:
            self.dsems[sem] = self.nc.alloc_semaphore("d_" + sem)
            self.dcnt[sem] = 0
        o = Op(eng, fn, is_dma=True, dsem=sem)
        o.needed = True
        return self._add(o, reads, writes)

    def flush(self, final=False):
        nc = self.nc
        for e in ENGS:
            comp = [o for o in self.cur[e] if not o.is_dma]
            if comp:
                comp[-1].needed = True
            for o in self.cur[e]:
                if o.is_dma:
                    self.dcnt[o.dsem] += 16
                    o.sig = (self.dsems[o.dsem], self.dcnt[o.dsem])
                elif o.needed:
                    self.ecnt[e] += 1
                    o.sig = (self.esem[e], self.ecnt[e])
            nxt = None
            for o in reversed(self.cur[e]):
                if o.is_dma:
                    continue
                if o.sig is not None:
                    nxt = o.sig
                else:
                    o.sig = nxt
        bar = []
        for e in ENGS:
            comp = [o for o in self.cur[e] if not o.is_dma]
            if comp:
                bar.append(comp[-1])
        lastd = {}
        for e in ENGS:
            for o in self.cur[e]:
                if o.is_dma:
                    lastd[o.dsem] = o
        bar.extend(lastd.values())
        if bar:
            self.bar = bar
            self.bar_pending = {e: True for e in ENGS}
        cur = self.cur
        sched = self

        def emit(e, engine):
            seen = sched.seen[e]
            for o in cur[e]:
                for d in o.deps:
                    sem, val = d.sig
                    key = id(sem)
                    if seen.get(key, 0) >= val:
                        continue
                    seen[key] = val
                    engine.wait_ge(sem, val)
                ins = o.fn(engine)
                if o.is_dma:
                    ins.then_inc(o.sig[0], 16)
                elif o.needed:
                    ins.then_inc(o.sig[0], 1)
            if final and e == "sync":
                for name, sem in sched.dsems.items():
                    if sched.dcnt[name] > 0:
                        engine.wait_ge(sem, sched.dcnt[name])

        with nc.Block() as block:
            if cur["sync"] or final:
                block.sync(lambda eng: emit("sync", eng))
            if cur["tensor"]:
                block.tensor(lambda eng: emit("tensor", eng))
            if cur["vector"]:
                block.vector(lambda eng: emit("vector", eng))
            if cur["scalar"]:
                block.scalar(lambda eng: emit("scalar", eng))
            if cur["gpsimd"]:
                block.gpsimd(lambda eng: emit("gpsimd", eng))
        self.cur = {e: [] for e in ENGS}


def build(stage=99, debug=False):
    nc = bass.Bass("TRN2", target_bir_lowering=False)

    def din(name, shape):
        return nc.dram_tensor(name, list(shape), F32, kind="ExternalInput").ap()

    x_own = din("x_own", [NTOK, D])
    x_pre = din("x_pre", [NTOK, D])
    mem = din("mem", [256, D])
    w_in = din("w_in", [D, 3592])
    w_out = din("w_out", [D, D])
    w_xq = din("w_xq", [D, D])
    w_xkv = din("w_xkv", [D, 2 * D])
    w_xo = din("w_xo", [D, D])
    w_gate = din("w_gate", [D, DFF])
    w_up = din("w_up", [D, DFF])
    w_down = din("w_down", [DFF, D])
    gvec = din("gvec", [128, 32])
    rowc = din("rowc", [1, 1160])
    cst = din("cst", [128, 32 * 64])
    d2t = din("d2t", [128, 8 * 128])
    qdt = din("qdt", [128, 4 * 128])
    cdt = din("cdt", [128, 4 * 64])
    kdt = din("kdt", [128, 8])
    pbv = din("pbv", [128, 1])
    rt4 = din("rt4", [128, 4 * 512])
    y = nc.dram_tensor("y", [NTOK, D], F32, kind="ExternalOutput").ap()
    dbg = nc.dram_tensor("dbg", [128, 4096], F32, kind="ExternalOutput").ap() if debug else None

    S = Sched(nc)

    def sb(name, shape, dt=F32):
        return nc.alloc_sbuf_tensor(name, list(shape), dt)

    identf = sb("identf", [128, 128])
    ident = sb("ident", [128, 128], BF16)
    trif = sb("trif", [128, 128])
    onesf = sb("onesf", [128, 128])
    onesb = sb("onesb", [128, 128], BF16)
    tmb = sb("tmb", [128, 128], BF16)
    gv = sb("gv", [128, 32])
    rc = sb("rc", [128, 1160])
    gk = sb("gk", [128, 64])
    gxk = sb("gxk", [128, 256])
    d2 = sb("d2", [128, 8, 128])
    qd = sb("qd", [128, 4, 128])
    cd = sb("cd", [128, 4, 64])
    kd = sb("kd", [128, 8])
    pb = sb("pb", [128, 1])
    zf = sb("zf", [128, 32, 8])
    Gs = sb("Gs", [128, 32, 8])
    Gt = sb("Gt", [128, 32, 8])
    Sf = sb("Sf", [128, 4, 64])
    Sbf = sb("Sbf", [128, 4, 64], BF16)
    sm = sb("sm", [128, 64])
    ARENA_W = 47552
    arena = sb("arena", [128, ARENA_W])

    def V(off, shape, dt=F32, parts=128):
        n = 1
        for s_ in shape[1:]:
            n *= s_
        nbytes = n * (4 if dt == F32 else 2)
        assert off % 4 == 0 and nbytes % 4 == 0 and off + nbytes <= ARENA_W * 4, (off, shape)
        ap = arena[0:parts, off // 4:(off + nbytes) // 4]
        if dt != F32:
            ap = ap.bitcast(dt)
        if len(shape) == 3:
            ap = ap.rearrange("p (a b) -> p a b", a=shape[1])
        elif len(shape) == 4:
            ap = ap.rearrange("p (a b c) -> p a b c", a=shape[1], b=shape[2])
        return ap

    KB = 1024
    OX, OY = 0, 64 * KB
    OW = OY + 33280
    OM = OW + 42240
    OZ = OM + 16 * KB

    def ps(name, words=512):
        return nc.alloc_psum_tensor(name, [128, words], F32)

    P_sc = ps("P_sc", 1024)
    P_t = ps("P_t")
    P_a = ps("P_a")
    P_b = ps("P_b")
    P_c = ps("P_c")
    P_q = ps("P_q")
    P_o = ps("P_o")

    def pv(t, shape, dt=F32, parts=128):
        ap = t[0:parts, :]
        if dt != F32:
            ap = ap.bitcast(dt)
        n = 1
        for s_ in shape[1:]:
            n *= s_
        ap = ap[:, 0:n]
        if len(shape) == 3:
            ap = ap.rearrange("p (a b) -> p a b", a=shape[1])
        return ap

    def act(out, in_, func, reads, writes, **kw):
        return S.op("scalar", lambda e: e.activation(out=out, in_=in_, func=func, **kw), reads, writes)

    def tt(out, a, b, op, reads, writes, eng="vector"):
        return S.op(eng, lambda e: e.tensor_tensor(out=out, in0=a, in1=b, op=op), reads, writes)

    def ts(out, a, s1, s2, op0, op1, reads, writes, eng="vector"):
        if s2 is None:
            return S.op(eng, lambda e: e.tensor_scalar(out=out, in0=a, scalar1=s1, scalar2=None, op0=op0),
                        reads, writes)
        return S.op(eng, lambda e: e.tensor_scalar(out=out, in0=a, scalar1=s1, scalar2=s2, op0=op0, op1=op1),
                    reads, writes)

    def stt(out, a, sc, b, op0, op1, reads, writes, eng="vector"):
        return S.op(eng, lambda e: e.scalar_tensor_tensor(out=out, in0=a, scalar=sc, in1=b, op0=op0, op1=op1),
                    reads, writes)

    def cp(out, in_, reads, writes, eng="vector"):
        return S.op(eng, lambda e: e.tensor_copy(out=out, in_=in_), reads, writes)

    def red(out, in_, reads, writes):
        return S.op("vector", lambda e: e.tensor_reduce(out=out, in_=in_, axis=AX.X, op=ALU.add), reads, writes)

    def mm(out, lhsT, rhs, start, stop, reads, writes):
        return S.op("tensor", lambda e: e.matmul(out, lhsT=lhsT, rhs=rhs, start=start, stop=stop), reads, writes)

    def tr(out, in_, reads, writes):
        return S.op("tensor", lambda e: e.transpose(out=out, in_=in_, identity=ident[:]),
                    list(reads) + ["ident"], writes)

    def ld(out, in_, sem, writes, eng="sync", reads=()):
        return S.dma(eng, lambda e: e.dma_start(out=out, in_=in_), sem, reads=reads, writes=writes)

    def rstd_from(ss_ap, out_ap, n, reads, writes):
        act(out_ap, ss_ap, AF.Ln, reads, writes, scale=1.0 / n, bias=epsb[:])
        act(out_ap, out_ap, AF.Exp, writes, writes, scale=-0.5)

    epsb = sb("epsb", [128, 1])
    sel64 = sb("sel64", [128, 64])
    shiftb = sb("shiftb", [128, 128], BF16)
    oneb = sb("oneb", [128, 1])

    S.op("gpsimd", lambda e: e.memset(identf[:], 0.0), writes=["identf"])
    S.op("gpsimd", lambda e: e.affine_select(out=identf[:], in_=identf[:], pattern=[[-1, 128]],
                                             compare_op=ALU.not_equal, fill=1.0, base=0, channel_multiplier=1),
         reads=["identf"], writes=["identf"])
    cp(ident[:], identf[:], ["identf"], ["ident"])
    S.op("gpsimd", lambda e: e.memset(onesf[:], 1.0), writes=["onesf"])
    S.op("gpsimd", lambda e: e.memset(onesb[:], 1.0), writes=["onesb"])
    S.op("gpsimd", lambda e: e.memset(epsb[:], EPS), writes=["epsb"])
    S.op("gpsimd", lambda e: e.memset(oneb[:], 1.0), writes=["oneb"])
    S.op("gpsimd", lambda e: e.memset(sel64[:], 0.0), writes=["sel64"])
    S.op("gpsimd", lambda e: e.memset(sel64[64:65, :], 1.0), writes=["sel64"])
    S.op("gpsimd", lambda e: e.memset(shiftb[:], 0.0), writes=["shiftb"])
    cp(shiftb[0:64, 64:128], ident[0:64, 0:64], ["ident", "shiftb"], ["shiftb"])
    S.op("gpsimd", lambda e: e.affine_select(out=trif[:], in_=onesf[:], pattern=[[1, 128]],
                                             compare_op=ALU.is_ge, fill=0.0, base=0, channel_multiplier=-1),
         reads=["onesf"], writes=["trif"])
    S.op("gpsimd", lambda e: e.memset(identf[:], 0.0), reads=["ident"], writes=["identf"])
    S.op("gpsimd", lambda e: e.affine_select(out=identf[:], in_=identf[:], pattern=[[1, 128]],
                                             compare_op=ALU.is_ge, fill=NEG, base=0, channel_multiplier=-1),
         reads=["identf"], writes=["identf"])
    cp(tmb[:], identf[:], ["identf"], ["tmb"])
    ld(gv[:], gvec, "c0", ["gv"])
    ld(rc[:], rowc.partition_broadcast(128), "c1", ["rc"])
    ld(d2[:], d2t.rearrange("p (a b) -> p a b", a=8), "c2", ["d2"])
    ld(qd[:], qdt.rearrange("p (a b) -> p a b", a=4), "c3", ["qd"])
    ld(cd[:], cdt.rearrange("p (a b) -> p a b", a=4), "c4", ["cd"])
    ld(kd[:], kdt, "c5", ["kd"])
    ld(pb[:], pbv, "c6", ["pb"])
    stt(gk[:], rc[:, 512:576], 0.125, rc[:, 576:640], ALU.mult, ALU.mult, ["rc"], ["gk"])
    stt(gxk[:], rc[:, 648:904], 1.0 / 16, rc[:, 904:1160], ALU.mult, ALU.mult, ["rc"], ["gxk"])
    S.op("vector", lambda e: e.memset(Sf[:], 0.0), writes=["Sf"])
    S.op("vector", lambda e: e.memset(Sbf[:], 0.0), writes=["Sbf"])
    GM, GX, GME, GF = 0, 8, 16, 24

    def front(src, srck, gcol, hT, hTk, xb, xbk, rst, rstk, pre_scale, junk=None):
        front_a(src, srck, xb, xbk, rst, rstk, pre_scale, junk)
        front_b(gcol, hT, hTk)

    def front_b(gcol, hT, hTk):
        ptv = pv(P_t, [128, 8, 128], BF16)
        tt(hT, ptv, gv[:, gcol:gcol + 8].unsqueeze(2).to_broadcast([128, 8, 128]), ALU.mult,
           ["P_t", "gv"], [hTk])

    def front_a(src, srck, xb, xbk, rst, rstk, pre_scale, junk=None):
        if junk is None:
            act(xb, src, AF.Square, [srck], [xbk, "ssq"], accum_out=sm[:, 0:1])
        else:
            act(junk, src, AF.Square, [srck], ["junk", "ssq"], accum_out=sm[:, 0:1])
        rstd_from(sm[:, 0:1], rst, D, ["ssq", "epsb"], [rstk])
        if pre_scale:
            act(xb, src, AF.Copy, [srck, rstk], [xbk], scale=rst)
        else:
            cp(xb, src, [srck], [xbk])
        ptv = pv(P_t, [128, 8, 128], BF16)
        for c in range(8):
            tr(ptv[:, c, :], xb[:, c * 128:(c + 1) * 128], [xbk], ["P_t"])

    def proj(pst, psk, hT, hTk, w, wk, c0, n):
        for c in range(8):
            mm(pst[:, 0:n], hT[:, c, :], w[:, c, c0:c0 + n], c == 0, c == 7, [hTk, wk], [psk])

    wA = V(OX, [128, 8, 2048], BF16)
    mixT = V(OM, [128, 4, NTOK], BF16)
    csT = V(OZ + 24 * KB, [128, 32, 64])
    ld(csT, cst.rearrange("p (a b) -> p a b", a=32), "c7", ["csT"])
    for c in range(8):
        ld(wA[:, c, :], w_in[c * 128:(c + 1) * 128, 0:2048], "wA", ["wA%d" % c], eng="gpsimd")
    xts = [V(OZ + 0, [128, 1024]), V(OZ + 4 * KB, [128, 1024])]
    hTs = [V(OZ + 8 * KB, [128, 8, 128], BF16), V(OZ + 10 * KB, [128, 8, 128], BF16)]
    xbv = V(OZ + 12 * KB, [128, 1024], BF16)
    qkf = V(OZ + 14 * KB, [128, 2, 8, 64])
    rqkb = V(OZ + 18 * KB, [128, 2, 8, 64], BF16)
    rvb = V(OZ + 20 * KB, [128, 8, 64], BF16)
    kdb = V(OZ + 21 * KB, [128, 8, 64], BF16)
    QTr = V(OZ + 22 * KB, [128, 4, 128], BF16)
    QTd = V(OZ + 23 * KB, [128, 4, 128], BF16)
    KTr = V(OW + 0, [128, 4, 128], BF16)
    scm = V(OW + 1 * KB, [128, 8, 128], BF16)
    egt = V(OW + 3 * KB, [128, 512])
    sgt = V(OW + 5 * KB, [128, 512])
    tmpA = V(OW + 7 * KB, [128, 8, 64])
    tmpB = V(OW + 9 * KB, [128, 8, 64])
    retb = V(OW + 11 * KB, [128, 512], BF16)
    rot1 = V(OW + 12 * KB, [128, 16, 32])
    rot2 = V(OW + 14 * KB, [128, 16, 32])
    RT = V(OW + 16 * KB, [128, 4, 512])
    ld(RT, rt4.rearrange("p (a b) -> p a b", a=4), "c8", ["RT"])
    QT4 = [V(OW + 24 * KB + i * KB, [128, 4, 128], BF16) for i in range(4)]
    rstA = sb("rstA", [128, 2])
    nrst = sb("nrst", [128, 1])
    st8 = sb("st8", [128, 6, 8])

    def rotary(src, dst, nh, keys_r, keys_w, tile_idx):
        cosb = csT[:, tile_idx, 0:32].unsqueeze(1).to_broadcast([128, nh, 32])
        sinb = csT[:, tile_idx, 32:64].unsqueeze(1).to_broadcast([128, nh, 32])
        x1 = src[:, :, 0:32]
        x2 = src[:, :, 32:64]
        r1 = rot1[:, 0:nh, :]
        r2 = rot2[:, 0:nh, :]
        G = "gpsimd"
        tt(r1, x1, cosb, ALU.mult, keys_r + ["csT"], ["rot1"], eng=G)
        tt(r2, x2, sinb, ALU.mult, keys_r + ["csT"], ["rot2"], eng=G)
        tt(dst[:, :, 0:32], r1, r2, ALU.subtract, ["rot1", "rot2"], keys_w, eng=G)
        tt(r1, x1, sinb, ALU.mult, keys_r + ["csT"], ["rot1"], eng=G)
        tt(r2, x2, cosb, ALU.mult, keys_r + ["csT"], ["rot2"], eng=G)
        tt(dst[:, :, 32:64], r1, r2, ALU.add, ["rot1", "rot2"], keys_w, eng=G)

    import os
    qkfs = [qkf, V(OW + 28 * KB, [128, 2, 8, 64])]
    sgts = [sgt, V(OW + 32 * KB, [128, 512])]
    rvbs = [rvb, V(OW + 34 * KB, [128, 8, 64], BF16)]
    tilesA = list(range(2 * NT)) if stage >= 1 else []

    def xload(it):
        sl = it % 2
        xsrc = (x_own if it >= NT else x_pre)[(it % NT) * 128:(it % NT + 1) * 128, :]
        ld(xts[sl], xsrc, "xt%d" % sl, ["xt%d" % sl])

    junkA = V(OZ + 22 * KB, [128, 1024], BF16)
    junkB = V(OZ + 29 * KB, [128, 1024], BF16)
    cur_junk = [junkA]

    def FAa(it):
        sl = it % 2
        rst = rstA[:, sl:sl + 1]
        front_a(xts[sl], "xt%d" % sl, xbv, "xb", rst, "rst%d" % sl, False, junk=cur_junk[0])
        if it + 2 < 2 * NT:
            xload(it + 2)

    def FAb(it):
        sl = it % 2
        front_b(GM, hTs[sl], "hT%d" % sl)

    def FA(it):
        FAa(it)
        FAb(it)

    def PGA(it, g):
        own = it >= NT
        sl = it % 2
        rst = rstA[:, sl:sl + 1]
        hT, hTk = hTs[sl], "hT%d" % sl
        rk_ = ["rst%d" % sl]
        q_ = qkfs[sl]
        if g == "rq" and own:
            proj(P_a, "P_a", hT, hTk, wA, "wA7", 0, 512)
            act(q_[:, 0].rearrange("p a b -> p (a b)"), P_a[:, 0:512], AF.Copy, ["P_a"] + rk_, ["qkf0_%d" % sl], scale=rst)
        elif g == "rk":
            proj(P_b, "P_b", hT, hTk, wA, "wA7", 512, 512)
            act(q_[:, 1].rearrange("p a b -> p (a b)"), P_b[:, 0:512], AF.Copy, ["P_b"] + rk_, ["qkf1_%d" % sl], scale=rst)
        elif g == "rv":
            proj(P_c, "P_c", hT, hTk, wA, "wA7", 1024, 512)
            act(rvbs[sl].rearrange("p a b -> p (a b)"), P_c[:, 0:512], AF.Copy, ["P_c"] + rk_, ["rvb%d" % sl], scale=rst)
        elif g == "rg" and own:
            proj(P_a, "P_a", hT, hTk, wA, "wA7", 1536, 512)
            ts(nrst[:], rst, -1.0, None, ALU.mult, ALU.bypass, rk_, ["nrst"])
            act(egt, P_a[:, 0:512], AF.Exp, ["P_a", "nrst"], ["egt"], scale=nrst[:])
            act(egt, egt, AF.Ln, ["egt", "oneb"], ["egt"], bias=oneb[:])
            act(egt, egt, AF.Exp, ["egt"], ["egt"], scale=-1.0)

    def A1b(it):
        own = it >= NT
        sl = it % 2
        if own:
            rst = rstA[:, sl:sl + 1]
            stt(sgts[sl], P_a[:, 0:512], rst, egt, ALU.mult, ALU.mult, ["P_a", "egt", "rst%d" % sl], ["sgt%d" % sl])

    def ROT(it):
        own = it >= NT
        sl = it % 2
        q_ = qkfs[sl]
        if own:
            rotary(q_.rearrange("p a b c -> p (a b) c"), rqkb.rearrange("p a b c -> p (a b) c"), 16,
                   ["qkf0_%d" % sl, "qkf1_%d" % sl], ["rqkb"], it)
        else:
            rotary(q_[:, 1], rqkb[:, 1], 8, ["qkf1_%d" % sl], ["rqkb"], it)

    pqv = pv(P_q, [128, 8, 128], BF16)
    scv = pv(P_sc, [128, 8, 128])
    pov = pv(P_o, [128, 8, 64])
    pkv = pv(P_q, [128, 4, 128])

    def A2s1(it):
        own = it >= NT
        tt(kdb, rqkb[:, 1], kd[:].unsqueeze(2).to_broadcast([128, 8, 64]), ALU.mult, ["rqkb", "kd"], ["kdb"])
        if own:
            for p_ in range(4):
                tr(pqv[:, p_, :], rqkb[:, 0, 2 * p_:2 * p_ + 2, :].rearrange("p a b -> p (a b)"), ["rqkb"], ["P_q"])
            for p_ in range(4):
                tr(pqv[:, 4 + p_, :], rqkb[:, 1, 2 * p_:2 * p_ + 2, :].rearrange("p a b -> p (a b)"), ["rqkb"], ["P_q"])
            for i4 in range(4):
                tt(QT4[i4], pqv[:, 0:4, :], RT[:, i4, :].rearrange("p (a b) -> p a b", a=4), ALU.mult,
                   ["P_q", "RT"], ["QT4_%d" % i4])
            cp(KTr, pqv[:, 4:8, :], ["P_q"], ["KTr"])

    def A2s2(it):
        if it >= NT:
            for h in range(8):
                p_ = h // 2
                mm(scv[:, h, :], KTr[:, p_, :], QT4[h % 2][:, p_, :], True, True, ["KTr", "QT4_%d" % (h % 2)], ["P_sc"])
            tt(scm[:, 0:4, :], scv[:, 0:4, :], d2[:, 0:4, :], ALU.mult, ["P_sc", "d2"], ["scm"])
            tt(scm[:, 4:8, :], scv[:, 4:8, :], d2[:, 4:8, :], ALU.mult, ["P_sc", "d2"], ["scm"])

    def A2s3(it):
        own = it >= NT
        sl = it % 2
        rvb_ = rvbs[sl]
        rvk = "rvb%d" % sl
        if own:
            for h in range(8):
                p_ = h // 2
                mm(pov[:, h, :], scm[:, h, :], rvb_[:, h, :], True, False, ["scm", rvk], ["P_o"])
                mm(pov[:, h, :], QT4[2 + h % 2][:, p_, :], Sbf[:, p_, :], False, True, ["QT4_%d" % (2 + h % 2), "Sbf"], ["P_o"])
        for p_ in range(4):
            mm(pkv[:, p_, :], kdb[:, 2 * p_:2 * p_ + 2, :].rearrange("p a b -> p (a b)"),
               rvb_[:, 2 * p_:2 * p_ + 2, :].rearrange("p a b -> p (a b)"), True, True, ["kdb", rvk], ["P_q"])
        tt(Sf[:], Sf[:], cd[:], ALU.mult, ["Sf", "cd"], ["Sf"])
        tt(Sf[0:64], Sf[0:64], pkv[0:64, :, 0:64], ALU.add, ["Sf", "P_q"], ["Sf"])
        tt(Sf[64:128], Sf[64:128], pkv[64:128, :, 64:128], ALU.add, ["Sf", "P_q"], ["Sf"])
        cp(Sbf[:], Sf[:], ["Sf"], ["Sbf"])
        if own:
            s1, s2, mu, var, rg_, t0 = (st8[:, i, :] for i in range(6))
            red(s1, pov, ["P_o"], ["st_s1"])
            act(tmpA, pov, AF.Square, ["P_o"], ["tmpA"])
            red(s2, tmpA, ["tmpA"], ["st_s2"])
            ts(mu, s1, 1.0 / 64, None, ALU.mult, ALU.bypass, ["st_s1"], ["st_mu"])
            tt(t0, mu, mu, ALU.mult, ["st_mu"], ["st_t0"])
            stt(var, s2, 1.0 / 64, t0, ALU.mult, ALU.subtract, ["st_s2", "st_t0"], ["st_var"])
            act(rg_, var, AF.Ln, ["st_var", "epsb"], ["st_rg"], bias=epsb[:])
            act(rg_, rg_, AF.Exp, ["st_rg"], ["st_rg"], scale=-0.5)
            tt(tmpA, pov, mu.unsqueeze(2).to_broadcast([128, 8, 64]), ALU.subtract, ["P_o", "st_mu"], ["tmpA"])
            tt(tmpB, tmpA, rg_.unsqueeze(2).to_broadcast([128, 8, 64]), ALU.mult, ["tmpA", "st_rg"], ["tmpB"])

    def A2s4(it):
        if it >= NT:
            sl = it % 2
            tt(tmpA.rearrange("p a b -> p (a b)"), tmpB.rearrange("p a b -> p (a b)"), rc[:, 0:512], ALU.mult,
               ["tmpB", "rc"], ["tmpA"], eng="gpsimd")
            tt(retb, tmpA.rearrange("p a b -> p (a b)"), sgts[sl], ALU.mult, ["tmpA", "sgt%d" % sl], ["retb"], eng="gpsimd")

    def A2s5(it):
        if it >= NT:
            for c in range(4):
                tr(pqv[:, c, :], retb[:, c * 128:(c + 1) * 128], ["retb"], ["P_q"])
            t_ = it - NT
            cp(mixT[:, :, t_ * 128:(t_ + 1) * 128], pqv[:, 0:4, :], ["P_q"], ["mixT"])

    if tilesA:
        NA = 2 * NT
        xload(0)
        xload(1)
        FA(0)
        FA(1)
        for g in ("rq", "rk", "rv", "rg"):
            PGA(0, g)
        ROT(0)
        A1b(0)
        for k in range(NA):
            n = k + 1 if k + 1 < NA else None
            if k + 2 < NA:
                FA(k + 2)
            A2s1(k)
            if n is not None:
                PGA(n, "rq")
                PGA(n, "rk")
                ROT(n)
            A2s2(k)
            if k > 0:
                A2s5(k - 1)
            if n is not None:
                PGA(n, "rv")
            A2s3(k)
            if n is not None:
                PGA(n, "rg")
            A2s4(k)
            if n is not None:
                A1b(n)
        A2s5(NA - 1)
    S.flush()

    if debug and stage == 1:
        dbt = V(OW + 20 * KB, [128, 4096])
        cp(dbt[:, 0:2048], mixT[:, 0, :], ["mixT"], ["dbt"])
        cp(dbt[:, 2048:4096], mixT[:, 3, :], ["mixT"], ["dbt"])
        ld(dbg, dbt, "dbg", [], reads=["dbt"])

    KT = V(OX, [65, 8, 4096], BF16, parts=65)
    VA = V(OY, [128, 32, 8 * 65], BF16)
    wB = V(OW, [128, 8, 1544], BF16)
    Qn = V(OW + 25 * KB, [128, 16, 8 * 65], BF16)
    fqk = V(OZ + 14 * KB, [128, 2, 8, 64])
    sqf = V(OZ + 18 * KB, [128, 2, 8, 64])
    Kns = [V(OZ + 22 * KB, [128, 8, 65], BF16), V(OZ + 22 * KB + 1040, [128, 8, 65], BF16)]
    rh = sb("rh", [128, 16])
    if stage >= 2:
        for c in range(8):
            ld(wB[:, c, :], w_in[c * 128:(c + 1) * 128, 2048:3592], "wB", ["wB%d" % c], eng="gpsimd")
        S.op("gpsimd", lambda e: e.memset(VA, 1.0), writes=["VA"])
        for k_ in range(2):
            S.op("gpsimd", lambda e, k_=k_: e.memset(Kns[k_], 1.0), writes=["Kn%d" % k_])
    fqks = [fqk, V(OZ + 25 * KB, [128, 2, 8, 64])]

    def PGB(it, g):
        own = it >= NT
        sl = it % 2
        rst = rstA[:, sl:sl + 1]
        hT, hTk = hTs[sl], "hT%d" % sl
        rk_ = ["rst%d" % sl]
        f_ = fqks[sl]
        if g == "fq" and own:
            proj(P_a, "P_a", hT, hTk, wB, "wB7", 0, 512)
            act(f_[:, 0].rearrange("p a b -> p (a b)"), P_a[:, 0:512], AF.Copy, ["P_a"] + rk_, ["fqk0_%d" % sl], scale=rst)
        elif g == "fk":
            proj(P_b, "P_b", hT, hTk, wB, "wB7", 512, 512)
            act(f_[:, 1].rearrange("p a b -> p (a b)"), P_b[:, 0:512], AF.Copy, ["P_b"] + rk_, ["fqk1_%d" % sl], scale=rst)
        elif g == "fv":
            proj(P_c, "P_c", hT, hTk, wB, "wB7", 1024, 512)
            vav = VA[:, it, :].rearrange("p (a b) -> p a b", a=8)
            act(vav[:, :, 0:64], P_c[:, 0:512].rearrange("p (a b) -> p a b", a=8), AF.Copy, ["P_c"] + rk_, ["VA"], scale=rst)
        elif g == "ff":
            proj(P_o, "P_o", hT, hTk, wB, "wB7", 1536, 8)

    def B1b(it):
        sl = it % 2
        rst = rstA[:, sl:sl + 1]
        stt(zf[:, it, :], P_o[:, 0:8], rst, rc[:, 640:648], ALU.mult, ALU.add, ["P_o", "rc", "rst%d" % sl], ["zf"])

    def B2sq(it):
        own = it >= NT
        sl = it % 2
        f_ = fqks[sl]
        lo = 0 if own else 1
        src_ = f_.rearrange("p a b c -> p (a b) c")[:, lo * 8:16, :]
        sq_ = sqf.rearrange("p a b c -> p (a b) c")[:, lo * 8:16, :]
        rkeys = ["fqk0_%d" % sl, "fqk1_%d" % sl] if own else ["fqk1_%d" % sl]
        tt(sq_, src_, src_, ALU.mult, rkeys, ["sqf"], eng="gpsimd")

    def B2s1(it):
        own = it >= NT
        sl = it % 2
        f_ = fqks[sl]
        lo = 0 if own else 1
        src_ = f_.rearrange("p a b c -> p (a b) c")[:, lo * 8:16, :]
        sq_ = sqf.rearrange("p a b c -> p (a b) c")[:, lo * 8:16, :]
        red(rh[:, lo * 8:16], sq_, ["sqf"], ["rh"])
        act(rh[:, lo * 8:16], rh[:, lo * 8:16], AF.Ln, ["rh", "epsb"], ["rh"], scale=1.0 / 64, bias=epsb[:])
        act(rh[:, lo * 8:16], rh[:, lo * 8:16], AF.Exp, ["rh"], ["rh"], scale=-0.5)

    def B2s1b(it):
        own = it >= NT
        sl = it % 2
        f_ = fqks[sl]
        if own:
            qnv = Qn[:, it - NT, :].rearrange("p (a b) -> p a b", a=8)
            tt(qnv[:, :, 0:64], f_[:, 0], rh[:, 0:8].unsqueeze(2).to_broadcast([128, 8, 64]), ALU.mult,
               ["fqk0_%d" % sl, "rh"], ["Qn"])
        Kn, Knk = Kns[sl], "Kn%d" % sl
        tt(sqf[:, 1], f_[:, 1], rh[:, 8:16].unsqueeze(2).to_broadcast([128, 8, 64]), ALU.mult,
           ["fqk1_%d" % sl, "rh", "sqf"], ["sqf"])
        tt(Kn[:, :, 0:64], sqf[:, 1], gk[:].unsqueeze(1).to_broadcast([128, 8, 64]), ALU.mult, ["sqf", "gk"], [Knk])

    def B2s2(it):
        sl = it % 2
        Kn, Knk = Kns[sl], "Kn%d" % sl
        pkt = pv(P_q, [65, 8, 128], BF16, parts=65)
        for h in range(8):
            tr(pkt[:, h, :], Kn[:, h, :], [Knk], ["P_q"])
        cp(KT[:, :, it * 128:(it + 1) * 128], pkt, ["P_q"], ["KT"])

    if stage >= 2:
        NB = 2 * NT
        cur_junk[0] = junkB
        xload(0)
        xload(1)
        FA(0)
        FA(1)
        for g in ("fq", "fk", "fv", "ff"):
            PGB(0, g)
        B2sq(0)
        B1b(0)
        for k in range(NB):
            n = k + 1 if k + 1 < NB else None
            if k + 2 < NB:
                FAa(k + 2)
            B2s1(k)
            if k + 2 < NB:
                FAb(k + 2)
            B2s1b(k)
            if n is not None:
                PGB(n, "fq")
                PGB(n, "fk")
                B2sq(n)
                PGB(n, "fv")
            B2s2(k)
            if n is not None:
                PGB(n, "ff")
                B1b(n)
    if stage >= 2:
        zfl = zf[:].rearrange("p a b -> p (a b)")
        act(zfl, zfl, AF.Exp, ["zf"], ["zf"], scale=-1.0)
        act(zfl, zfl, AF.Ln, ["zf", "oneb"], ["zf"], bias=oneb[:])
        pg1 = P_a[:, 0:256]
        pg2 = P_b[:, 0:256]
        mm(pg1, trif[:], zfl, True, True, ["trif", "zf"], ["P_a"])
        mm(pg2, onesf[:], zfl, True, True, ["onesf", "zf"], ["P_b"])
        Gsl = Gs[:].rearrange("p a b -> p (a b)")
        cp(Gt[:].rearrange("p a b -> p (a b)"), pg2, ["P_b"], ["Gt"])
        cp(Gsl, pg1, ["P_a"], ["Gs"])
        bufs_ = [(Gt, "Gt"), (zf, "zf")]
        cur_ = 0
        for st_ in (1, 2, 4, 8, 16):
            (a_, ak), (b_, bk) = bufs_[cur_], bufs_[1 - cur_]
            tt(b_[:, st_:32, :], a_[:, st_:32, :], a_[:, 0:32 - st_, :], ALU.add, [ak], [bk])
            tt(b_[:, 0:st_, :], a_[:, 0:st_, :], a_[:, 0:st_, :], ALU.max, [ak], [bk])
            cur_ = 1 - cur_
        fin_, fink = bufs_[cur_]
        tt(Gs[:, 1:32, :], Gs[:, 1:32, :], fin_[:, 0:31, :], ALU.add, ["Gs", fink], ["Gs"])
        qn4 = Qn.rearrange("p a (b c) -> p a b c", b=8)
        ts(qn4[:, :, :, 64], Gs[:, 16:32, :], -1.0, None, ALU.mult, ALU.bypass, ["Gs"], ["Qn"])
        ts(Gs[:, 0:16, :], Gs[:, 0:16, :], pb[:], None, ALU.add, ALU.bypass, ["Gs", "pb"], ["Gs"])
    S.flush()

    QT = V(OZ, [65, 8, NTOK], BF16, parts=65)
    for t_ in range(NT if stage >= 2 else 0):
        pqt = pv(P_q if t_ % 2 == 0 else P_t, [65, 8, 128], BF16, parts=65)
        pk_ = "P_q" if t_ % 2 == 0 else "P_t"
        qnv = Qn[:, t_, :].rearrange("p (a b) -> p a b", a=8)
        for h in range(8):
            tr(pqt[:, h, :], qnv[:, h, :], ["Qn"], [pk_])
        cp(QT[:, :, t_ * 128:(t_ + 1) * 128], pqt, [pk_], ["QT"])
    S.flush()

    foxT2 = V(OW, [128, 4, NTOK], BF16)
    ftmp = V(OW + 16 * KB, [128, 512], BF16)
    pTs = [V(OW + 33 * KB + i * KB, [128, 512], BF16) for i in range(3)]
    osb = V(OW + 36 * KB, [128, 512])
    rl = V(OW + 38 * KB, [128, 512])
    wor = V(OW + 17 * KB, [128, 4, D], BF16)
    wof2 = V(OW + 25 * KB, [128, 4, D], BF16)
    if stage >= 4:
        for c in range(4):
            ld(wor[:, c, :], w_out[c * 128:(c + 1) * 128, :], "wor", ["wor%d" % c], eng="gpsimd")
        for p_ in range(4):
            ld(wof2[:, p_, :], w_out[512 + p_ * 128:512 + (p_ + 1) * 128, :], "wof", ["wof%d" % p_], eng="gpsimd")
    if stage >= 3:
        S.op("gpsimd", lambda e: e.memset(rl, 0.0), writes=["rl"])
    psS = [P_a, P_b, P_c]
    psk = ["P_a", "P_b", "P_c"]
    psO = [P_o, P_q]
    pok = ["P_o", "P_q"]
    tiles = []
    for h in range(8 if stage >= 3 else 0):
        for qi in range(4):
            nk = 16 + 4 * (qi + 1)
            for kt in range(nk):
                tiles.append((h, qi, kt, nk))

    def emit_S(idx):
        h, qi, kt, nk = tiles[idx]
        j = kt - (16 + 4 * qi)
        q0 = 128 * j if j > 0 else 0
        ps_, psk_ = psS[idx % 3], psk[idx % 3]
        diag = j >= 0
        mm(ps_[:, q0:512], KT[:, h, kt * 128:(kt + 1) * 128], QT[:, h, qi * 512 + q0:(qi + 1) * 512],
           True, not diag, ["KT", "QT"], [psk_])
        if diag:
            mm(ps_[:, q0:q0 + 128], ident[:], tmb[:], False, True, ["ident", "tmb"], [psk_])

    def make_norm(h, qi, po_, pk_o):
        def fn():
            S.op("vector", lambda e: e.reciprocal(out=rl[64:65, :], in_=po_[64:65, 0:512]), [pk_o], ["rl"])
            mm(P_t[0:64, 0:512], sel64[:], rl, True, True, ["sel64", "rl"], ["P_t"])
            cp(osb[0:64, :], po_[0:64, 0:512], [pk_o], ["osb"])
            cols = slice(qi * 512, (qi + 1) * 512)
            if h % 2 == 0:
                tt(foxT2[0:64, h // 2, cols], osb[0:64, :], P_t[0:64, 0:512], ALU.mult, ["osb", "P_t"], ["foxT"])
            else:
                tt(ftmp[0:64, :], osb[0:64, :], P_t[0:64, 0:512], ALU.mult, ["osb", "P_t"], ["ftmp"])
                mm(P_t[:, 0:512], shiftb[0:64, :], ftmp[0:64, :], True, True, ["shiftb", "ftmp"], ["P_t"])
                cp(foxT2[64:128, h // 2, cols], P_t[64:128, 0:512], ["P_t"], ["foxT"])
        return fn

    LOOK = 2
    for i0 in range(min(LOOK, len(tiles))):
        emit_S(i0)
    pending = None
    since = 0
    for idx, (h, qi, kt, nk) in enumerate(tiles):
        if idx + LOOK < len(tiles):
            emit_S(idx + LOOK)
        g = h * 4 + qi
        po_, pk_o = psO[g % 2], pok[g % 2]
        j = kt - (16 + 4 * qi)
        q0 = 128 * j if j > 0 else 0
        ps_, psk_ = psS[idx % 3], psk[idx % 3]
        pT, pTk = pTs[idx % 3], "pT%d" % (idx % 3)
        act(pT[:, q0:512], ps_[:, q0:512], AF.Exp, [psk_, "Gs"], [pTk], bias=Gs[:, kt, h:h + 1])
        vav = VA[:, kt, :].rearrange("p (a b) -> p a b", a=8)
        mm(po_[0:65, q0:512], vav[:, h, :], pT[:, q0:512], kt == 0, kt == nk - 1, ["VA", pTk], [pk_o])
        since += 1
        if pending is not None and since >= 6:
            pending()
            pending = None
        if kt == nk - 1:
            if pending is not None:
                pending()
            pending = make_norm(h, qi, po_, pk_o)
            since = 0
    if pending is not None:
        pending()
    S.flush()

    if debug and stage == 3:
        dbt = V(OZ, [128, 4096])
        S.op("vector", lambda e: e.memset(dbt, 0.0), ["QT"], ["dbt"])
        cp(dbt[0:64, 0:2048], foxT2[0:64, 0, :], ["foxT"], ["dbt"])
        cp(dbt[64:128, 2048:4096], foxT2[64:128, 3, :], ["foxT"], ["dbt"])
        ld(dbg, dbt, "dbg", [], reads=["dbt"])

    hres = V(OX, [128, NT, D])
    wq = V(OY, [128, 8, D], BF16)
    wo = V(OY + 16 * KB, [128, 8, D], BF16)
    if stage >= 5:
        for c in range(8):
            ld(wq[:, c, :], w_xq[c * 128:(c + 1) * 128, :], "wq", ["wq%d" % c], eng="gpsimd")
        for c in range(8):
            ld(wo[:, c, :], w_xo[c * 128:(c + 1) * 128, :], "wo", ["wo%d" % c], eng="gpsimd")
    x3 = [V(OZ + 0, [128, 1024]), V(OZ + 4 * KB, [128, 1024])]
    for t_ in range(NT if stage >= 4 else 0):
        sl = t_ % 2
        ld(x3[sl], x_own[t_ * 128:(t_ + 1) * 128, :], "xt%d" % sl, ["x3_%d" % sl])
        for half in range(2):
            ps_, psk_ = (P_a, "P_a") if half == 0 else (P_b, "P_b")
            for c in range(4):
                mm(ps_[:, 0:512], mixT[:, c, t_ * 128:(t_ + 1) * 128], wor[:, c, half * 512:(half + 1) * 512],
                   c == 0, False, ["mixT", "wor3"], [psk_])
            for p_ in range(4):
                mm(ps_[:, 0:512], foxT2[:, p_, t_ * 128:(t_ + 1) * 128], wof2[:, p_, half * 512:(half + 1) * 512],
                   False, p_ == 3, ["foxT", "wof3"], [psk_])
            tt(hres[:, t_, half * 512:(half + 1) * 512], x3[sl][:, half * 512:(half + 1) * 512], ps_[:, 0:512], ALU.add,
               ["x3_%d" % sl, psk_], ["h%d" % t_])
    S.flush()

    wq = V(OY, [128, 8, D], BF16)
    wo = V(OY + 16 * KB, [128, 8, D], BF16)
    wkv = V(OW, [128, 8, 2 * D], BF16)
    kTx = V(OM, [128, 8, 256], BF16)
    vbx = V(OM + 4 * KB, [128, 2, D], BF16)
    kfx = V(OM + 8 * KB, [128, D])
    ksq = V(OM + 12 * KB, [128, D])
    mt4 = [V(OZ + 0, [128, 1024]), V(OZ + 4 * KB, [128, 1024])]
    xb4 = V(OZ + 8 * KB, [128, 1024], BF16)
    hT4 = V(OZ + 10 * KB, [128, 8, 512], BF16)
    mT4 = V(OZ + 18 * KB, [128, 8, 256], BF16)
    kbx = V(OZ + 22 * KB, [128, D], BF16)
    qT4 = V(OW + 0, [128, 8, 512], BF16)
    sq4 = V(OW + 8 * KB, [128, 8, 512], BF16)
    rq4 = V(OW + 16 * KB, [128, 512])
    pT4 = [V(OW + 18 * KB, [128, 2, 512], BF16), V(OW + 20 * KB, [128, 2, 512], BF16)]
    oT4 = V(OW + 22 * KB, [128, 8, 512], BF16)
    rl4 = V(OW + 30 * KB, [128, 512])
    ou4 = V(OW + 32 * KB, [128, 512])
    rs4 = sb("rs4", [128, 8])
    if stage >= 5:
        for c in range(8):
            ld(wkv[:, c, :], w_xkv[c * 128:(c + 1) * 128, :], "wkv", ["wkv%d" % c], eng="gpsimd")
        for mtile in range(2):
            ld(mt4[mtile], mem[mtile * 128:(mtile + 1) * 128, :], "xt%d" % mtile, ["mt%d" % mtile])
            front(mt4[mtile], "mt%d" % mtile, GME, mT4[:, :, mtile * 128:(mtile + 1) * 128], "mT4", xb4, "xb4",
                  rs4[:, 0:1], "rs4a", True)
            for half in range(2):
                for c in range(8):
                    mm(P_a[:, 0:512], mT4[:, c, mtile * 128:(mtile + 1) * 128], wkv[:, c, half * 512:(half + 1) * 512],
                       c == 0, c == 7, ["mT4", "wkv7"], ["P_a"])
                act(kfx[:, half * 512:(half + 1) * 512], P_a[:, 0:512], AF.Copy, ["P_a"], ["kfx"])
            for half in range(2):
                for c in range(8):
                    mm(P_b[:, 0:512], mT4[:, c, mtile * 128:(mtile + 1) * 128],
                       wkv[:, c, D + half * 512:D + (half + 1) * 512], c == 0, c == 7, ["mT4", "wkv7"], ["P_b"])
                act(vbx[:, mtile, half * 512:(half + 1) * 512], P_b[:, 0:512], AF.Copy, ["P_b"], ["vbx"])
            tt(ksq, kfx, kfx, ALU.mult, ["kfx"], ["ksq"])
            red(rs4[:, 4:8], ksq.rearrange("p (a b) -> p a b", a=4), ["ksq"], ["rs4k"])
            act(rs4[:, 4:8], rs4[:, 4:8], AF.Ln, ["rs4k", "epsb"], ["rs4k"], scale=1.0 / 256, bias=epsb[:])
            act(rs4[:, 4:8], rs4[:, 4:8], AF.Exp, ["rs4k"], ["rs4k"], scale=-0.5)
            tt(ksq.rearrange("p (a b) -> p a b", a=4), kfx.rearrange("p (a b) -> p a b", a=4),
               rs4[:, 4:8].unsqueeze(2).to_broadcast([128, 4, 256]), ALU.mult, ["kfx", "rs4k", "ksq"], ["ksq"])
            tt(kbx.rearrange("p (a b) -> p a b", a=4), ksq.rearrange("p (a b) -> p a b", a=4),
               gxk[:].unsqueeze(1).to_broadcast([128, 4, 256]), ALU.mult, ["ksq", "gxk"], ["kbx"])
            pkx = pv(P_q, [128, 8, 128], BF16)
            for c in range(8):
                tr(pkx[:, c, :], kbx[:, c * 128:(c + 1) * 128], ["kbx"], ["P_q"])
            cp(kTx[:, :, mtile * 128:(mtile + 1) * 128], pkx, ["P_q"], ["kTx"])
    S.flush()
    hT4s = [V(OZ + 0, [128, 8, 512], BF16), V(OZ + 10 * KB, [128, 8, 512], BF16)]
    xb4s = [V(OZ + 8 * KB, [128, 1024], BF16), V(OZ + 18 * KB, [128, 1024], BF16)]
    junk4 = V(OZ + 20 * KB, [128, 1024], BF16)
    rq4s = [V(OZ + 24 * KB + i * 2 * KB, [128, 512]) for i in range(4)]
    pT4s = [V(OW + 16 * KB + i * 2 * KB, [128, 2, 512], BF16) for i in range(4)]
    oT4 = V(OW + 24 * KB, [128, 8, 512], BF16)
    rl4s = [V(OW + 32 * KB + i * 2 * KB, [128, 512]) for i in range(2)]

    def F4(b):
        for ti in range(4):
            t_ = b * 4 + ti
            k_ = ti % 2
            front(hres[:, t_, :], "h%d" % t_, GX, hT4s[b % 2][:, :, ti * 128:(ti + 1) * 128], "hT4_%d" % (b % 2),
                  xb4s[k_], "xb4_%d" % k_, rs4[:, 1 + k_:2 + k_], "rs4b%d" % k_, True, junk=junk4)

    def Q4(b):
        hT, hk = hT4s[b % 2], "hT4_%d" % (b % 2)
        for c2 in range(8):
            ps_, psk_ = (P_a, "P_a") if c2 % 2 == 0 else (P_b, "P_b")
            for c in range(8):
                mm(ps_[:, 0:512], wq[:, c, c2 * 128:(c2 + 1) * 128], hT[:, c, :], c == 0, c == 7, ["wq7", hk], [psk_])
            cp(qT4[:, c2, :], ps_[:, 0:512], [psk_], ["qT4_%d" % c2])
            tt(sq4[:, c2, :], qT4[:, c2, :], qT4[:, c2, :], ALU.mult, ["qT4_%d" % c2], ["sq4_%d" % c2], eng="gpsimd")

    def H4(b):
        for hd in range(4):
            mm(P_c[:, 0:512], onesb[:], sq4[:, 2 * hd, :], True, False, ["onesb", "sq4_%d" % (2 * hd)], ["P_c"])
            mm(P_c[:, 0:512], onesb[:], sq4[:, 2 * hd + 1, :], False, True, ["onesb", "sq4_%d" % (2 * hd + 1)], ["P_c"])
            rk = "rq4_%d" % hd
            act(rq4s[hd], P_c[:, 0:512], AF.Ln, ["P_c", "epsb"], [rk], scale=1.0 / 256, bias=epsb[:])
            act(rq4s[hd], rq4s[hd], AF.Exp, [rk], [rk], scale=-0.5)
            for half in range(2):
                c2 = 2 * hd + half
                tt(qT4[:, c2, :], qT4[:, c2, :], rq4s[hd], ALU.mult, ["qT4_%d" % c2, rk], ["qT4_%d" % c2])
        for hd in range(4):
            pT_, pTk = pT4s[hd], "pT4_%d" % hd
            for mtile in range(2):
                ps_, psk_ = (P_a, "P_a") if mtile == 0 else (P_b, "P_b")
                for half in range(2):
                    c2 = 2 * hd + half
                    mm(ps_[:, 0:512], kTx[:, c2, mtile * 128:(mtile + 1) * 128], qT4[:, c2, :], half == 0, half == 1,
                       ["kTx", "qT4_%d" % c2], [psk_])
                act(pT_[:, mtile, :], ps_[:, 0:512], AF.Exp, [psk_], [pTk])
        for hd in range(4):
            pT_, pTk = pT4s[hd], "pT4_%d" % hd
            rl_, rlk = rl4s[hd % 2], "rl4_%d" % (hd % 2)
            mm(P_c[:, 0:512], onesb[:], pT_[:, 0, :], True, False, ["onesb", pTk], ["P_c"])
            mm(P_c[:, 0:512], onesb[:], pT_[:, 1, :], False, True, ["onesb", pTk], ["P_c"])
            act(rl_, P_c[:, 0:512], AF.Ln, ["P_c"], [rlk])
            act(rl_, rl_, AF.Exp, [rlk], [rlk], scale=-1.0)
            for half in range(2):
                c2 = 2 * hd + half
                po_, pk_o = (P_o, "P_o") if half == 0 else (P_q, "P_q")
                for mtile in range(2):
                    mm(po_[:, 0:512], vbx[:, mtile, c2 * 128:(c2 + 1) * 128], pT_[:, mtile, :], mtile == 0, mtile == 1,
                       ["vbx", pTk], [pk_o])
                tt(oT4[:, c2, :], po_[:, 0:512], rl_, ALU.mult, [pk_o, rlk], ["oT4"])

    def O4(b):
        for ti in range(4):
            t_ = b * 4 + ti
            for half in range(2):
                ps_, psk_ = (P_a, "P_a") if half == 0 else (P_b, "P_b")
                for c in range(8):
                    mm(ps_[:, 0:512], oT4[:, c, ti * 128:(ti + 1) * 128], wo[:, c, half * 512:(half + 1) * 512],
                       c == 0, c == 7, ["oT4", "wo7"], [psk_])
                tt(hres[:, t_, half * 512:(half + 1) * 512], hres[:, t_, half * 512:(half + 1) * 512], ps_[:, 0:512],
                   ALU.add, ["h%d" % t_, psk_], ["h%d" % t_])

    if stage >= 5:
        F4(0)
        Q4(0)
        for b_ in range(4):
            if b_ + 1 < 4:
                F4(b_ + 1)
            H4(b_)
            if b_ + 1 < 4:
                Q4(b_ + 1)
            O4(b_)
    S.flush()

    hn5 = V(OZ, [128, 8, NTOK], BF16)
    R0 = OY
    PARTS = [6, 6, 5, 5]

    def wslot(s_):
        base = R0 + s_ * 36 * KB
        return (V(base, [128, 8, 768], BF16), V(base + 12 * KB, [128, 8, 768], BF16),
                V(base + 24 * KB, [128, 6, D], BF16))

    aT = V(R0 + 72 * KB, [128, 6, 512], BF16)
    sg5 = V(R0 + 78 * KB, [128, 512])
    xb5 = V(R0 + 80 * KB, [128, 1024], BF16)
    rs5 = sb("rs5", [128, 1])
    xb5s = [xb5, V(R0 + 82 * KB, [128, 1024], BF16)]
    junk5 = V(R0 + 84 * KB, [128, 1024], BF16)
    rs5b = sb("rs5b", [128, 2])

    last_key = {}

    def load_part(pi):
        nf = PARTS[pi]
        f0 = sum(PARTS[:pi])
        wg_, wu_, wd_ = wslot(pi % 2)
        sk = "ws%d" % (pi % 2)
        i_ = [0]

        def one(dst, src):
            keys = ["%s_%d_%d" % (sk, pi, i_[0])]
            if i_[0] == 0 and pi >= 2:
                keys.append(last_key[pi - 2])
            ld(dst, src, sk, keys, eng="gpsimd")
            i_[0] += 1
            return keys[0]
        last = None
        for c in range(8):
            one(wg_[:, c, 0:nf * 128], w_gate[c * 128:(c + 1) * 128, f0 * 128:(f0 + nf) * 128])
            one(wu_[:, c, 0:nf * 128], w_up[c * 128:(c + 1) * 128, f0 * 128:(f0 + nf) * 128])
        for f in range(nf):
            last = one(wd_[:, f, :], w_down[(f0 + f) * 128:(f0 + f + 1) * 128, :])
        last_key[pi] = last
        return [last]

    def ffn_fronts(blk):
        for t_ in range(blk * 4, blk * 4 + 4):
            k_ = t_ % 2
            front(hres[:, t_, :], "h%d" % t_, GF, hn5[:, :, t_ * 128:(t_ + 1) * 128], "hn5_%d" % (t_ // 4),
                  xb5s[k_], "xb5_%d" % k_, rs5b[:, k_:k_ + 1], "rs5_%d" % k_, True, junk=junk5)

    def ffn_part(pi, wkeys):
        nf = PARTS[pi]
        wg_, wu_, wd_ = wslot(pi % 2)
        for blk in range(4):
            if pi == 0 and blk + 1 < 4:
                ffn_fronts(blk + 1)
            hk = "hn5_%d" % blk
            for f in range(nf):
                for c in range(8):
                    mm(P_a[:, 0:512], wg_[:, c, f * 128:(f + 1) * 128], hn5[:, c, blk * 512:(blk + 1) * 512],
                       c == 0, c == 7, wkeys + [hk], ["P_a"])
                for c in range(8):
                    mm(P_b[:, 0:512], wu_[:, c, f * 128:(f + 1) * 128], hn5[:, c, blk * 512:(blk + 1) * 512],
                       c == 0, c == 7, wkeys + [hk], ["P_b"])
                act(sg5, P_a[:, 0:512], AF.Silu, ["P_a"], ["sg5"])
                tt(aT[:, f, :], sg5, P_b[:, 0:512], ALU.mult, ["sg5", "P_b"], ["aT%d" % f])
            for ti in range(4):
                t_ = blk * 4 + ti
                for half in range(2):
                    ps_, psk_ = (P_c, "P_c") if half == 0 else (P_o, "P_o")
                    for f in range(nf):
                        mm(ps_[:, 0:512], aT[:, f, ti * 128:(ti + 1) * 128], wd_[:, f, half * 512:(half + 1) * 512],
                           f == 0, f == nf - 1, ["aT%d" % f] + wkeys, [psk_])
                    tt(hres[:, t_, half * 512:(half + 1) * 512], hres[:, t_, half * 512:(half + 1) * 512],
                       ps_[:, 0:512], ALU.add, ["h%d" % t_, psk_], ["h%d" % t_])

    if stage >= 6:
        wk0 = load_part(0)
        wk1 = load_part(1)
        ffn_fronts(0)
        ffn_part(0, wk0)
        wk2 = load_part(2)
        ffn_part(1, wk1)
        wk3 = load_part(3)
        ffn_part(2, wk2)
        ffn_part(3, wk3)
    if stage >= 4:
        for t_ in range(NT):
            ld(y[t_ * 128:(t_ + 1) * 128, :], hres[:, t_, :], "yout", [], reads=["h%d" % t_])
    S.flush(final=True)
    return nc


def _consts(s):
    H = 8
    idx = np.arange(128, dtype=np.float32)
    log_g = np.log(1.0 - 2.0 ** (-5.0 - np.arange(H, dtype=np.float32))).astype(np.float32)
    inv_freq = (10000.0 ** (-np.arange(0, 64, 2, dtype=np.float32) / 64)).astype(np.float32)
    pos_own = np.arange(NTOK, dtype=np.float32) + s * NTOK
    pos_pre = np.arange(NTOK, dtype=np.float32)
    pos = np.concatenate([pos_pre, pos_own]).astype(np.float32)
    ang = (pos[:, None] * inv_freq[None, :]).astype(np.float32)
    cs = np.concatenate([np.cos(ang), np.sin(ang)], axis=1).astype(np.float32)
    cst = cs.reshape(32, 128, 64).transpose(1, 0, 2).reshape(128, 32 * 64)
    i_ = idx[None, :]
    j_ = idx[:, None]
    same = (i_ // 64) == (j_ // 64)
    lower = (i_ >= 64) & (j_ < 64)
    d2 = np.zeros((128, H, 128), np.float32)
    for h in range(H):
        a = np.exp(log_g[h] * np.abs(i_ - j_)).astype(np.float32)
        b = np.exp(log_g[h] * (i_ - j_)).astype(np.float32)
        d2[:, h, :] = np.where(same, a, np.where(lower, b, 0.0)) * 0.125
    qd = np.zeros((128, 4, 128), np.float32)
    cd = np.zeros((128, 4, 64), np.float32)
    for p in range(128):
        for pr in range(4):
            h = 2 * pr + p // 64
            qd[p, pr, :] = np.exp(log_g[h] * (idx + 1.0)) * 0.125
            cd[p, pr, :] = np.exp(log_g[h] * 128.0)
    kd = np.exp(log_g[None, :] * (127.0 - idx[:, None])).astype(np.float32)
    me = np.zeros((128, 4, 128), np.float32); me[0:64] = 1.0
    mo = np.zeros((128, 4, 128), np.float32); mo[64:128] = 1.0
    rt4 = np.concatenate([me.reshape(128, -1), mo.reshape(128, -1), (qd * me).reshape(128, -1),
                          (qd * mo).reshape(128, -1)], axis=1).astype(np.float32)
    return dict(rt4=np.ascontiguousarray(rt4), cst=np.ascontiguousarray(cst), d2t=d2.reshape(128, -1), qdt=qd.reshape(128, -1),
                cdt=cd.reshape(128, -1), kdt=kd)


_NC_CACHE = {}


def kernel(x, mem, g_mix, w_in, b_forget, g_ret_out, g_fox_q, g_fox_k, w_out, g_xattn, w_xq, w_xkv, g_mem,
           g_xq, g_xk, w_xo, g_ffn, w_gate, w_up, w_down, _stage=99, _debug=False):
    f = lambda a: np.ascontiguousarray(np.asarray(a, dtype=np.float32))
    x, mem = f(x), f(mem)
    key = (_stage, _debug)
    if key not in _NC_CACHE:
        _NC_CACHE[key] = build(_stage, _debug)
    nc = _NC_CACHE[key]
    gcol = lambda g: f(g).reshape(8, 128).T
    gvec = np.ascontiguousarray(np.concatenate([gcol(g_mix[0]), gcol(g_xattn[0]), gcol(g_mem[0]), gcol(g_ffn[0])], axis=1))
    rowc = np.concatenate([f(g_ret_out[0]).reshape(-1), f(g_fox_q[0]), f(g_fox_k[0]), f(b_forget[0]),
                           f(g_xq[0]), f(g_xk[0])])[None, :]
    shared = dict(w_in=f(w_in[0]), w_out=f(w_out[0]), w_xq=f(w_xq[0]), w_xkv=f(w_xkv[0]), w_xo=f(w_xo[0]),
                  w_gate=f(w_gate[0]), w_up=f(w_up[0]), w_down=f(w_down[0]), gvec=gvec, rowc=f(rowc))
    cs = [_consts(0), _consts(1)]
    zeros = np.zeros((NTOK, D), np.float32)
    in_maps = []
    for core in range(8):
        b, s = core // 2, core % 2
        m = dict(shared)
        m.update(cs[s])
        m["x_own"] = np.ascontiguousarray(x[b, s * NTOK:(s + 1) * NTOK])
        m["x_pre"] = np.ascontiguousarray(x[b, 0:NTOK]) if s == 1 else zeros
        m["mem"] = mem[b]
        m["pbv"] = np.full((128, 1), 0.0 if s == 1 else NEG, np.float32)
        in_maps.append(m)
    res = run_bass_kernel_spmd(nc, in_maps, core_ids=list(range(8)))
    out = np.empty((4, 4096, D), np.float32)
    for core in range(8):
        b, s = core // 2, core % 2
        out[b, s * NTOK:(s + 1) * NTOK] = res.results[core]["y"]
    if _debug:
        return out, [r["dbg"] for r in res.results]
    return out
```

```python
import numpy as np
import concourse.bass as bass
import concourse.mybir as mybir
from concourse.bass_utils import run_bass_kernel_spmd

F32 = mybir.dt.float32
BF16 = mybir.dt.bfloat16
ALU = mybir.AluOpType
AF = mybir.ActivationFunctionType
AX = mybir.AxisListType

ENGS = ("tensor", "vector", "scalar", "gpsimd", "sync")
D = 1024
NT = 16
NTOK = 2048
DFF = 2816
EPS = 1e-6
NEG = -30000.0


class Op:
    __slots__ = ("eng", "fn", "deps", "sig", "is_dma", "dsem", "needed")

    def __init__(self, eng, fn, is_dma=False, dsem=None):
        self.eng = eng
        self.fn = fn
        self.deps = []
        self.sig = None
        self.is_dma = is_dma
        self.dsem = dsem
        self.needed = False


class Sched:
    def __init__(self, nc):
        self.nc = nc
        self.esem = {e: nc.alloc_semaphore("s_" + e) for e in ENGS}
        self.ecnt = {e: 0 for e in ENGS}
        self.dsems = {}
        self.dcnt = {}
        self.cur = {e: [] for e in ENGS}
        self.lastw = {}
        self.readers = {}
        self.seen = {e: {} for e in ENGS}
        self.bar = []
        self.bar_pending = {e: False for e in ENGS}

    def _add(self, o, reads, writes):
        deps = []
        if self.bar_pending[o.eng]:
            self.bar_pending[o.eng] = False
            o.deps.extend(self.bar)
        for k in reads:
            w = self.lastw.get(k)
            if w is not None:
                deps.append(w)
        for k in writes:
            w = self.lastw.get(k)
            if w is not None:
                deps.append(w)
            deps.extend(self.readers.get(k, ()))
        for k in reads:
            self.readers.setdefault(k, []).append(o)
        for k in writes:
            self.lastw[k] = o
            self.readers[k] = []
        seen = set()
        for d in deps:
            if d is o or id(d) in seen:
                continue
            seen.add(id(d))
            if d.eng == "tensor" and o.eng == "tensor" and not d.is_dma and not o.is_dma:
                continue
            o.deps.append(d)
            d.needed = True
        self.cur[o.eng].append(o)
        return o

    def op(self, eng, fn, reads=(), writes=()):
        return self._add(Op(eng, fn), reads, writes)

    def dma(self, eng, fn, sem, reads=(), writes=()):
        if sem not in self.dsems:
            self.dsems[sem] = self.nc.alloc_semaphore("d_" + sem)
            self.dcnt[sem] = 0
        o = Op(eng, fn, is_dma=True, dsem=sem)
        o.needed = True
        return self._add(o, reads, writes)

    def flush(self, final=False):
        nc = self.nc
        for e in ENGS:
            comp = [o for o in self.cur[e] if not o.is_dma]
            if comp:
                comp[-1].needed = True
            for o in self.cur[e]:
                if o.is_dma:
                    self.dcnt[o.dsem] += 16
                    o.sig = (self.dsems[o.dsem], self.dcnt[o.dsem])
                elif o.needed:
                    self.ecnt[e] += 1
                    o.sig = (self.esem[e], self.ecnt[e])
            nxt = None
            for o in reversed(self.cur[e]):
                if o.is_dma:
                    continue
                if o.sig is not None:
                    nxt = o.sig
                else:
                    o.sig = nxt
        bar = []
        for e in ENGS:
            comp = [o for o in self.cur[e] if not o.is_dma]
            if comp:
                bar.append(comp[-1])
        lastd = {}
        for e in ENGS:
            for o in self.cur[e]:
                if o.is_dma:
                    lastd[o.dsem] = o
        bar.extend(lastd.values())
        if bar:
            self.bar = bar
            self.bar_pending = {e: True for e in ENGS}
        cur = self.cur
        sched = self

        def emit(e, engine):
            seen = sched.seen[e]
            for o in cur[e]:
                for d in o.deps:
                    sem, val = d.sig
                    key = id(sem)
                    if seen.get(key, 0) >= val:
                        continue
                    seen[key] = val
                    engine.wait_ge(sem, val)
                ins = o.fn(engine)
                if o.is_dma:
                    ins.then_inc(o.sig[0], 16)
                elif o.needed:
                    ins.then_inc(o.sig[0], 1)
            if final and e == "sync":
                for name, sem in sched.dsems.items():
                    if sched.dcnt[name] > 0:
                        engine.wait_ge(sem, sched.dcnt[name])

        with nc.Block() as block:
            if cur["sync"] or final:
                block.sync(lambda eng: emit("sync", eng))
            if cur["tensor"]:
                block.tensor(lambda eng: emit("tensor", eng))
            if cur["vector"]:
                block.vector(lambda eng: emit("vector", eng))
            if cur["scalar"]:
                block.scalar(lambda eng: emit("scalar", eng))
            if cur["gpsimd"]:
                block.gpsimd(lambda eng: emit("gpsimd", eng))
        self.cur = {e: [] for e in ENGS}


def build(stage=99, debug=False):
    nc = bass.Bass("TRN2", target_bir_lowering=False)

    def din(name, shape):
        return nc.dram_tensor(name, list(shape), F32, kind="ExternalInput").ap()

    x_own = din("x_own", [NTOK, D])
    x_pre = din("x_pre", [NTOK, D])
    mem = din("mem", [256, D])
    w_in = din("w_in", [D, 3592])
    w_out = din("w_out", [D, D])
    w_xq = din("w_xq", [D, D])
    w_xkv = din("w_xkv", [D, 2 * D])
    w_xo = din("w_xo", [D, D])
    w_gate = din("w_gate", [D, DFF])
    w_up = din("w_up", [D, DFF])
    w_down = din("w_down", [DFF, D])
    gvec = din("gvec", [128, 32])
    rowc = din("rowc", [1, 1160])
    cst = din("cst", [128, 32 * 64])
    d2t = din("d2t", [128, 8 * 128])
    qdt = din("qdt", [128, 4 * 128])
    cdt = din("cdt", [128, 4 * 64])
    kdt = din("kdt", [128, 8])
    pbv = din("pbv", [128, 1])
    rt4 = din("rt4", [128, 4 * 512])
    y = nc.dram_tensor("y", [NTOK, D], F32, kind="ExternalOutput").ap()
    dbg = nc.dram_tensor("dbg", [128, 4096], F32, kind="ExternalOutput").ap() if debug else None

    S = Sched(nc)

    def sb(name, shape, dt=F32):
        return nc.alloc_sbuf_tensor(name, list(shape), dt)

    identf = sb("identf", [128, 128])
    ident = sb("ident", [128, 128], BF16)
    trif = sb("trif", [128, 128])
    onesf = sb("onesf", [128, 128])
    onesb = sb("onesb", [128, 128], BF16)
    tmb = sb("tmb", [128, 128], BF16)
    gv = sb("gv", [128, 32])
    rc = sb("rc", [128, 1160])
    gk = sb("gk", [128, 64])
    gxk = sb("gxk", [128, 256])
    d2 = sb("d2", [128, 8, 128])
    qd = sb("qd", [128, 4, 128])
    cd = sb("cd", [128, 4, 64])
    kd = sb("kd", [128, 8])
    pb = sb("pb", [128, 1])
    zf = sb("zf", [128, 32, 8])
    Gs = sb("Gs", [128, 32, 8])
    Gt = sb("Gt", [128, 32, 8])
    Sf = sb("Sf", [128, 4, 64])
    Sbf = sb("Sbf", [128, 4, 64], BF16)
    sm = sb("sm", [128, 64])
    ARENA_W = 47552
    arena = sb("arena", [128, ARENA_W])

    def V(off, shape, dt=F32, parts=128):
        n = 1
        for s_ in shape[1:]:
            n *= s_
        nbytes = n * (4 if dt == F32 else 2)
        assert off % 4 == 0 and nbytes % 4 == 0 and off + nbytes <= ARENA_W * 4, (off, shape)
        ap = arena[0:parts, off // 4:(off + nbytes) // 4]
        if dt != F32:
            ap = ap.bitcast(dt)
        if len(shape) == 3:
            ap = ap.rearrange("p (a b) -> p a b", a=shape[1])
        elif len(shape) == 4:
            ap = ap.rearrange("p (a b c) -> p a b c", a=shape[1], b=shape[2])
        return ap

    KB = 1024
    OX, OY = 0, 64 * KB
    OW = OY + 33280
    OM = OW + 42240
    OZ = OM + 16 * KB

    def ps(name, words=512):
        return nc.alloc_psum_tensor(name, [128, words], F32)

    P_sc = ps("P_sc", 1024)
    P_t = ps("P_t")
    P_a = ps("P_a")
    P_b = ps("P_b")
    P_c = ps("P_c")
    P_q = ps("P_q")
    P_o = ps("P_o")

    def pv(t, shape, dt=F32, parts=128):
        ap = t[0:parts, :]
        if dt != F32:
            ap = ap.bitcast(dt)
        n = 1
        for s_ in shape[1:]:
            n *= s_
        ap = ap[:, 0:n]
        if len(shape) == 3:
            ap = ap.rearrange("p (a b) -> p a b", a=shape[1])
        return ap

    def act(out, in_, func, reads, writes, **kw):
        return S.op("scalar", lambda e: e.activation(out=out, in_=in_, func=func, **kw), reads, writes)

    def tt(out, a, b, op, reads, writes, eng="vector"):
        return S.op(eng, lambda e: e.tensor_tensor(out=out, in0=a, in1=b, op=op), reads, writes)

    def ts(out, a, s1, s2, op0, op1, reads, writes, eng="vector"):
        if s2 is None:
            return S.op(eng, lambda e: e.tensor_scalar(out=out, in0=a, scalar1=s1, scalar2=None, op0=op0),
                        reads, writes)
        return S.op(eng, lambda e: e.tensor_scalar(out=out, in0=a, scalar1=s1, scalar2=s2, op0=op0, op1=op1),
                    reads, writes)

    def stt(out, a, sc, b, op0, op1, reads, writes, eng="vector"):
        return S.op(eng, lambda e: e.scalar_tensor_tensor(out=out, in0=a, scalar=sc, in1=b, op0=op0, op1=op1),
                    reads, writes)

    def cp(out, in_, reads, writes, eng="vector"):
        return S.op(eng, lambda e: e.tensor_copy(out=out, in_=in_), reads, writes)

    def red(out, in_, reads, writes):
        return S.op("vector", lambda e: e.tensor_reduce(out=out, in_=in_, axis=AX.X, op=ALU.add), reads, writes)

    def mm(out, lhsT, rhs, start, stop, reads, writes):
        return S.op("tensor", lambda e: e.matmul(out, lhsT=lhsT, rhs=rhs, start=start, stop=stop), reads, writes)

    def tr(out, in_, reads, writes):
        return S.op("tensor", lambda e: e.transpose(out=out, in_=in_, identity=ident[:]),
                    list(reads) + ["ident"], writes)

    def ld(out, in_, sem, writes, eng="sync", reads=()):
        return S.dma(eng, lambda e: e.dma_start(out=out, in_=in_), sem, reads=reads, writes=writes)

    def rstd_from(ss_ap, out_ap, n, reads, writes):
        act(out_ap, ss_ap, AF.Ln, reads, writes, scale=1.0 / n, bias=epsb[:])
        act(out_ap, out_ap, AF.Exp, writes, writes, scale=-0.5)

    epsb = sb("epsb", [128, 1])
    sel64 = sb("sel64", [128, 64])
    shiftb = sb("shiftb", [128, 128], BF16)
    oneb = sb("oneb", [128, 1])

    S.op("gpsimd", lambda e: e.memset(identf[:], 0.0), writes=["identf"])
    S.op("gpsimd", lambda e: e.affine_select(out=identf[:], in_=identf[:], pattern=[[-1, 128]],
                                             compare_op=ALU.not_equal, fill=1.0, base=0, channel_multiplier=1),
         reads=["identf"], writes=["identf"])
    cp(ident[:], identf[:], ["identf"], ["ident"])
    S.op("gpsimd", lambda e: e.memset(onesf[:], 1.0), writes=["onesf"])
    S.op("gpsimd", lambda e: e.memset(onesb[:], 1.0), writes=["onesb"])
    S.op("gpsimd", lambda e: e.memset(epsb[:], EPS), writes=["epsb"])
    S.op("gpsimd", lambda e: e.memset(oneb[:], 1.0), writes=["oneb"])
    S.op("gpsimd", lambda e: e.memset(sel64[:], 0.0), writes=["sel64"])
    S.op("gpsimd", lambda e: e.memset(sel64[64:65, :], 1.0), writes=["sel64"])
    S.op("gpsimd", lambda e: e.memset(shiftb[:], 0.0), writes=["shiftb"])
    cp(shiftb[0:64, 64:128], ident[0:64, 0:64], ["ident", "shiftb"], ["shiftb"])
    S.op("gpsimd", lambda e: e.affine_select(out=trif[:], in_=onesf[:], pattern=[[1, 128]],
                                             compare_op=ALU.is_ge, fill=0.0, base=0, channel_multiplier=-1),
         reads=["onesf"], writes=["trif"])
    S.op("gpsimd", lambda e: e.memset(identf[:], 0.0), reads=["ident"], writes=["identf"])
    S.op("gpsimd", lambda e: e.affine_select(out=identf[:], in_=identf[:], pattern=[[1, 128]],
                                             compare_op=ALU.is_ge, fill=NEG, base=0, channel_multiplier=-1),
         reads=["identf"], writes=["identf"])
    cp(tmb[:], identf[:], ["identf"], ["tmb"])
    ld(gv[:], gvec, "c0", ["gv"])
    ld(rc[:], rowc.partition_broadcast(128), "c1", ["rc"])
    ld(d2[:], d2t.rearrange("p (a b) -> p a b", a=8), "c2", ["d2"])
    ld(qd[:], qdt.rearrange("p (a b) -> p a b", a=4), "c3", ["qd"])
    ld(cd[:], cdt.rearrange("p (a b) -> p a b", a=4), "c4", ["cd"])
    ld(kd[:], kdt, "c5", ["kd"])
    ld(pb[:], pbv, "c6", ["pb"])
    stt(gk[:], rc[:, 512:576], 0.125, rc[:, 576:640], ALU.mult, ALU.mult, ["rc"], ["gk"])
    stt(gxk[:], rc[:, 648:904], 1.0 / 16, rc[:, 904:1160], ALU.mult, ALU.mult, ["rc"], ["gxk"])
    S.op("vector", lambda e: e.memset(Sf[:], 0.0), writes=["Sf"])
    S.op("vector", lambda e: e.memset(Sbf[:], 0.0), writes=["Sbf"])
    GM, GX, GME, GF = 0, 8, 16, 24

    def front(src, srck, gcol, hT, hTk, xb, xbk, rst, rstk, pre_scale, junk=None):
        front_a(src, srck, xb, xbk, rst, rstk, pre_scale, junk)
        front_b(gcol, hT, hTk)

    def front_b(gcol, hT, hTk):
        ptv = pv(P_t, [128, 8, 128], BF16)
        tt(hT, ptv, gv[:, gcol:gcol + 8].unsqueeze(2).to_broadcast([128, 8, 128]), ALU.mult,
           ["P_t", "gv"], [hTk])

    def front_a(src, srck, xb, xbk, rst, rstk, pre_scale, junk=None):
        if junk is None:
            act(xb, src, AF.Square, [srck], [xbk, "ssq"], accum_out=sm[:, 0:1])
        else:
            act(junk, src, AF.Square, [srck], ["junk", "ssq"], accum_out=sm[:, 0:1])
        rstd_from(sm[:, 0:1], rst, D, ["ssq", "epsb"], [rstk])
        if pre_scale:
            act(xb, src, AF.Copy, [srck, rstk], [xbk], scale=rst)
        else:
            cp(xb, src, [srck], [xbk])
        ptv = pv(P_t, [128, 8, 128], BF16)
        for c in range(8):
            tr(ptv[:, c, :], xb[:, c * 128:(c + 1) * 128], [xbk], ["P_t"])

    def proj(pst, psk, hT, hTk, w, wk, c0, n):
        for c in range(8):
            mm(pst[:, 0:n], hT[:, c, :], w[:, c, c0:c0 + n], c == 0, c == 7, [hTk, wk], [psk])

    wA = V(OX, [128, 8, 2048], BF16)
    mixT = V(OM, [128, 4, NTOK], BF16)
    csT = V(OZ + 24 * KB, [128, 32, 64])
    ld(csT, cst.rearrange("p (a b) -> p a b", a=32), "c7", ["csT"])
    for c in range(8):
        ld(wA[:, c, :], w_in[c * 128:(c + 1) * 128, 0:2048], "wA", ["wA%d" % c], eng="gpsimd")
    xts = [V(OZ + 0, [128, 1024]), V(OZ + 4 * KB, [128, 1024])]
    hTs = [V(OZ + 8 * KB, [128, 8, 128], BF16), V(OZ + 10 * KB, [128, 8, 128], BF16)]
    xbv = V(OZ + 12 * KB, [128, 1024], BF16)
    qkf = V(OZ + 14 * KB, [128, 2, 8, 64])
    rqkb = V(OZ + 18 * KB, [128, 2, 8, 64], BF16)
    rvb = V(OZ + 20 * KB, [128, 8, 64], BF16)
    kdb = V(OZ + 21 * KB, [128, 8, 64], BF16)
    QTr = V(OZ + 22 * KB, [128, 4, 128], BF16)
    QTd = V(OZ + 23 * KB, [128, 4, 128], BF16)
    KTr = V(OW + 0, [128, 4, 128], BF16)
    scm = V(OW + 1 * KB, [128, 8, 128], BF16)
    egt = V(OW + 3 * KB, [128, 512])
    sgt = V(OW + 5 * KB, [128, 512])
    tmpA = V(OW + 7 * KB, [128, 8, 64])
    tmpB = V(OW + 9 * KB, [128, 8, 64])
    retb = V(OW + 11 * KB, [128, 512], BF16)
    rot1 = V(OW + 12 * KB, [128, 16, 32])
    rot2 = V(OW + 14 * KB, [128, 16, 32])
    RT = V(OW + 16 * KB, [128, 4, 512])
    ld(RT, rt4.rearrange("p (a b) -> p a b", a=4), "c8", ["RT"])
    QT4 = [V(OW + 24 * KB + i * KB, [128, 4, 128], BF16) for i in range(4)]
    rstA = sb("rstA", [128, 2])
    nrst = sb("nrst", [128, 1])
    st8 = sb("st8", [128, 6, 8])

    def rotary(src, dst, nh, keys_r, keys_w, tile_idx):
        cosb = csT[:, tile_idx, 0:32].unsqueeze(1).to_broadcast([128, nh, 32])
        sinb = csT[:, tile_idx, 32:64].unsqueeze(1).to_broadcast([128, nh, 32])
        x1 = src[:, :, 0:32]
        x2 = src[:, :, 32:64]
        r1 = rot1[:, 0:nh, :]
        r2 = rot2[:, 0:nh, :]
        G = "gpsimd"
        tt(r1, x1, cosb, ALU.mult, keys_r + ["csT"], ["rot1"], eng=G)
        tt(r2, x2, sinb, ALU.mult, keys_r + ["csT"], ["rot2"], eng=G)
        tt(dst[:, :, 0:32], r1, r2, ALU.subtract, ["rot1", "rot2"], keys_w, eng=G)
        tt(r1, x1, sinb, ALU.mult, keys_r + ["csT"], ["rot1"], eng=G)
        tt(r2, x2, cosb, ALU.mult, keys_r + ["csT"], ["rot2"], eng=G)
        tt(dst[:, :, 32:64], r1, r2, ALU.add, ["rot1", "rot2"], keys_w, eng=G)

    import os
    qkfs = [qkf, V(OW + 28 * KB, [128, 2, 8, 64])]
    sgts = [sgt, V(OW + 32 * KB, [128, 512])]
    rvbs = [rvb, V(OW + 34 * KB, [128, 8, 64], BF16)]
    tilesA = list(range(2 * NT)) if stage >= 1 else []

    def xload(it):
        sl = it % 2
        xsrc = (x_own if it >= NT else x_pre)[(it % NT) * 128:(it % NT + 1) * 128, :]
        ld(xts[sl], xsrc, "xt%d" % sl, ["xt%d" % sl])

    junkA = V(OZ + 22 * KB, [128, 1024], BF16)
    junkB = V(OZ + 29 * KB, [128, 1024], BF16)
    cur_junk = [junkA]

    def FAa(it):
        sl = it % 2
        rst = rstA[:, sl:sl + 1]
        front_a(xts[sl], "xt%d" % sl, xbv, "xb", rst, "rst%d" % sl, False, junk=cur_junk[0])
        if it + 2 < 2 * NT:
            xload(it + 2)

    def FAb(it):
        sl = it % 2
        front_b(GM, hTs[sl], "hT%d" % sl)

    def FA(it):
        FAa(it)
        FAb(it)

    def PGA(it, g):
        own = it >= NT
        sl = it % 2
        rst = rstA[:, sl:sl + 1]
        hT, hTk = hTs[sl], "hT%d" % sl
        rk_ = ["rst%d" % sl]
        q_ = qkfs[sl]
        if g == "rq" and own:
            proj(P_a, "P_a", hT, hTk, wA, "wA7", 0, 512)
            act(q_[:, 0].rearrange("p a b -> p (a b)"), P_a[:, 0:512], AF.Copy, ["P_a"] + rk_, ["qkf0_%d" % sl], scale=rst)
        elif g == "rk":
            proj(P_b, "P_b", hT, hTk, wA, "wA7", 512, 512)
            act(q_[:, 1].rearrange("p a b -> p (a b)"), P_b[:, 0:512], AF.Copy, ["P_b"] + rk_, ["qkf1_%d" % sl], scale=rst)
        elif g == "rv":
            proj(P_c, "P_c", hT, hTk, wA, "wA7", 1024, 512)
            act(rvbs[sl].rearrange("p a b -> p (a b)"), P_c[:, 0:512], AF.Copy, ["P_c"] + rk_, ["rvb%d" % sl], scale=rst)
        elif g == "rg" and own:
            proj(P_a, "P_a", hT, hTk, wA, "wA7", 1536, 512)
            ts(nrst[:], rst, -1.0, None, ALU.mult, ALU.bypass, rk_, ["nrst"])
            act(egt, P_a[:, 0:512], AF.Exp, ["P_a", "nrst"], ["egt"], scale=nrst[:])
            act(egt, egt, AF.Ln, ["egt", "oneb"], ["egt"], bias=oneb[:])
            act(egt, egt, AF.Exp, ["egt"], ["egt"], scale=-1.0)

    def A1b(it):
        own = it >= NT
        sl = it % 2
        if own:
            rst = rstA[:, sl:sl + 1]
            stt(sgts[sl], P_a[:, 0:512], rst, egt, ALU.mult, ALU.mult, ["P_a", "egt", "rst%d" % sl], ["sgt%d" % sl])

    def ROT(it):
        own = it >= NT
        sl = it % 2
        q_ = qkfs[sl]
        if own:
            rotary(q_.rearrange("p a b c -> p (a b) c"), rqkb.rearrange("p a b c -> p (a b) c"), 16,
                   ["qkf0_%d" % sl, "qkf1_%d" % sl], ["rqkb"], it)
        else:
            rotary(q_[:, 1], rqkb[:, 1], 8, ["qkf1_%d" % sl], ["rqkb"], it)

    pqv = pv(P_q, [128, 8, 128], BF16)
    scv = pv(P_sc, [128, 8, 128])
    pov = pv(P_o, [128, 8, 64])
    pkv = pv(P_q, [128, 4, 128])

    def A2s1(it):
        own = it >= NT
        tt(kdb, rqkb[:, 1], kd[:].unsqueeze(2).to_broadcast([128, 8, 64]), ALU.mult, ["rqkb", "kd"], ["kdb"])
        if own:
            for p_ in range(4):
                tr(pqv[:, p_, :], rqkb[:, 0, 2 * p_:2 * p_ + 2, :].rearrange("p a b -> p (a b)"), ["rqkb"], ["P_q"])
            for p_ in range(4):
                tr(pqv[:, 4 + p_, :], rqkb[:, 1, 2 * p_:2 * p_ + 2, :].rearrange("p a b -> p (a b)"), ["rqkb"], ["P_q"])
            for i4 in range(4):
                tt(QT4[i4], pqv[:, 0:4, :], RT[:, i4, :].rearrange("p (a b) -> p a b", a=4), ALU.mult,
                   ["P_q", "RT"], ["QT4_%d" % i4])
            cp(KTr, pqv[:, 4:8, :], ["P_q"], ["KTr"])

    def A2s2(it):
        if it >= NT:
            for h in range(8):
                p_ = h // 2
                mm(scv[:, h, :], KTr[:, p_, :], QT4[h % 2][:, p_, :], True, True, ["KTr", "QT4_%d" % (h % 2)], ["P_sc"])
            tt(scm[:, 0:4, :], scv[:, 0:4, :], d2[:, 0:4, :], ALU.mult, ["P_sc", "d2"], ["scm"])
            tt(scm[:, 4:8, :], scv[:, 4:8, :], d2[:, 4:8, :], ALU.mult, ["P_sc", "d2"], ["scm"])

    def A2s3(it):
        own = it >= NT
        sl = it % 2
        rvb_ = rvbs[sl]
        rvk = "rvb%d" % sl
        if own:
            for h in range(8):
                p_ = h // 2
                mm(pov[:, h, :], scm[:, h, :], rvb_[:, h, :], True, False, ["scm", rvk], ["P_o"])
                mm(pov[:, h, :], QT4[2 + h % 2][:, p_, :], Sbf[:, p_, :], False, True, ["QT4_%d" % (2 + h % 2), "Sbf"], ["P_o"])
        for p_ in range(4):
            mm(pkv[:, p_, :], kdb[:, 2 * p_:2 * p_ + 2, :].rearrange("p a b -> p (a b)"),
               rvb_[:, 2 * p_:2 * p_ + 2, :].rearrange("p a b -> p (a b)"), True, True, ["kdb", rvk], ["P_q"])
        tt(Sf[:], Sf[:], cd[:], ALU.mult, ["Sf", "cd"], ["Sf"])
        tt(Sf[0:64], Sf[0:64], pkv[0:64, :, 0:64], ALU.add, ["Sf", "P_q"], ["Sf"])
        tt(Sf[64:128], Sf[64:128], pkv[64:128, :, 64:128], ALU.add, ["Sf", "P_q"], ["Sf"])
        cp(Sbf[:], Sf[:], ["Sf"], ["Sbf"])
        if own:
            s1, s2, mu, var, rg_, t0 = (st8[:, i, :] for i in range(6))
            red(s1, pov, ["P_o"], ["st_s1"])
            act(tmpA, pov, AF.Square, ["P_o"], ["tmpA"])
            red(s2, tmpA, ["tmpA"], ["st_s2"])
            ts(mu, s1, 1.0 / 64, None, ALU.mult, ALU.bypass, ["st_s1"], ["st_mu"])
            tt(t0, mu, mu, ALU.mult, ["st_mu"], ["st_t0"])
            stt(var, s2, 1.0 / 64, t0, ALU.mult, ALU.subtract, ["st_s2", "st_t0"], ["st_var"])
            act(rg_, var, AF.Ln, ["st_var", "epsb"], ["st_rg"], bias=epsb[:])
            act(rg_, rg_, AF.Exp, ["st_rg"], ["st_rg"], scale=-0.5)
            tt(tmpA, pov, mu.unsqueeze(2).to_broadcast([128, 8, 64]), ALU.subtract, ["P_o", "st_mu"], ["tmpA"])
            tt(tmpB, tmpA, rg_.unsqueeze(2).to_broadcast([128, 8, 64]), ALU.mult, ["tmpA", "st_rg"], ["tmpB"])

    def A2s4(it):
        if it >= NT:
            sl = it % 2
            tt(tmpA.rearrange("p a b -> p (a b)"), tmpB.rearrange("p a b -> p (a b)"), rc[:, 0:512], ALU.mult,
               ["tmpB", "rc"], ["tmpA"], eng="gpsimd")
            tt(retb, tmpA.rearrange("p a b -> p (a b)"), sgts[sl], ALU.mult, ["tmpA", "sgt%d" % sl], ["retb"], eng="gpsimd")

    def A2s5(it):
        if it >= NT:
            for c in range(4):
                tr(pqv[:, c, :], retb[:, c * 128:(c + 1) * 128], ["retb"], ["P_q"])
            t_ = it - NT
            cp(mixT[:, :, t_ * 128:(t_ + 1) * 128], pqv[:, 0:4, :], ["P_q"], ["mixT"])

    if tilesA:
        NA = 2 * NT
        xload(0)
        xload(1)
        FA(0)
        FA(1)
        for g in ("rq", "rk", "rv", "rg"):
            PGA(0, g)
        ROT(0)
        A1b(0)
        for k in range(NA):
            n = k + 1 if k + 1 < NA else None
            if k + 2 < NA:
                FA(k + 2)
            A2s1(k)
            if n is not None:
                PGA(n, "rq")
                PGA(n, "rk")
                ROT(n)
            A2s2(k)
            if k > 0:
                A2s5(k - 1)
            if n is not None:
                PGA(n, "rv")
            A2s3(k)
            if n is not None:
                PGA(n, "rg")
            A2s4(k)
            if n is not None:
                A1b(n)
        A2s5(NA - 1)
    S.flush()

    if debug and stage == 1:
        dbt = V(OW + 20 * KB, [128, 4096])
        cp(dbt[:, 0:2048], mixT[:, 0, :], ["mixT"], ["dbt"])
        cp(dbt[:, 2048:4096], mixT[:, 3, :], ["mixT"], ["dbt"])
        ld(dbg, dbt, "dbg", [], reads=["dbt"])

    KT = V(OX, [65, 8, 4096], BF16, parts=65)
    VA = V(OY, [128, 32, 8 * 65], BF16)
    wB = V(OW, [128, 8, 1544], BF16)
    Qn = V(OW + 25 * KB, [128, 16, 8 * 65], BF16)
    fqk = V(OZ + 14 * KB, [128, 2, 8, 64])
    sqf = V(OZ + 18 * KB, [128, 2, 8, 64])
    Kns = [V(OZ + 22 * KB, [128, 8, 65], BF16), V(OZ + 22 * KB + 1040, [128, 8, 65], BF16)]
    rh = sb("rh", [128, 16])
    if stage >= 2:
        for c in range(8):
            ld(wB[:, c, :], w_in[c * 128:(c + 1) * 128, 2048:3592], "wB", ["wB%d" % c], eng="gpsimd")
        S.op("gpsimd", lambda e: e.memset(VA, 1.0), writes=["VA"])
        for k_ in range(2):
            S.op("gpsimd", lambda e, k_=k_: e.memset(Kns[k_], 1.0), writes=["Kn%d" % k_])
    fqks = [fqk, V(OZ + 25 * KB, [128, 2, 8, 64])]

    def PGB(it, g):
        own = it >= NT
        sl = it % 2
        rst = rstA[:, sl:sl + 1]
        hT, hTk = hTs[sl], "hT%d" % sl
        rk_ = ["rst%d" % sl]
        f_ = fqks[sl]
        if g == "fq" and own:
            proj(P_a, "P_a", hT, hTk, wB, "wB7", 0, 512)
            act(f_[:, 0].rearrange("p a b -> p (a b)"), P_a[:, 0:512], AF.Copy, ["P_a"] + rk_, ["fqk0_%d" % sl], scale=rst)
        elif g == "fk":
            proj(P_b, "P_b", hT, hTk, wB, "wB7", 512, 512)
            act(f_[:, 1].rearrange("p a b -> p (a b)"), P_b[:, 0:512], AF.Copy, ["P_b"] + rk_, ["fqk1_%d" % sl], scale=rst)
        elif g == "fv":
            proj(P_c, "P_c", hT, hTk, wB, "wB7", 1024, 512)
            vav = VA[:, it, :].rearrange("p (a b) -> p a b", a=8)
            act(vav[:, :, 0:64], P_c[:, 0:512].rearrange("p (a b) -> p a b", a=8), AF.Copy, ["P_c"] + rk_, ["VA"], scale=rst)
        elif g == "ff":
            proj(P_o, "P_o", hT, hTk, wB, "wB7", 1536, 8)

    def B1b(it):
        sl = it % 2
        rst = rstA[:, sl:sl + 1]
        stt(zf[:, it, :], P_o[:, 0:8], rst, rc[:, 640:648], ALU.mult, ALU.add, ["P_o", "rc", "rst%d" % sl], ["zf"])

    def B2sq(it):
        own = it >= NT
        sl = it % 2
        f_ = fqks[sl]
        lo = 0 if own else 1
        src_ = f_.rearrange("p a b c -> p (a b) c")[:, lo * 8:16, :]
        sq_ = sqf.rearrange("p a b c -> p (a b) c")[:, lo * 8:16, :]
        rkeys = ["fqk0_%d" % sl, "fqk1_%d" % sl] if own else ["fqk1_%d" % sl]
        tt(sq_, src_, src_, ALU.mult, rkeys, ["sqf"], eng="gpsimd")

    def B2s1(it):
        own = it >= NT
        sl = it % 2
        f_ = fqks[sl]
        lo = 0 if own else 1
        src_ = f_.rearrange("p a b c -> p (a b) c")[:, lo * 8:16, :]
        sq_ = sqf.rearrange("p a b c -> p (a b) c")[:, lo * 8:16, :]
        red(rh[:, lo * 8:16], sq_, ["sqf"], ["rh"])
        act(rh[:, lo * 8:16], rh[:, lo * 8:16], AF.Ln, ["rh", "epsb"], ["rh"], scale=1.0 / 64, bias=epsb[:])
        act(rh[:, lo * 8:16], rh[:, lo * 8:16], AF.Exp, ["rh"], ["rh"], scale=-0.5)

    def B2s1b(it):
        own = it >= NT
        sl = it % 2
        f_ = fqks[sl]
        if own:
            qnv = Qn[:, it - NT, :].rearrange("p (a b) -> p a b", a=8)
            tt(qnv[:, :, 0:64], f_[:, 0], rh[:, 0:8].unsqueeze(2).to_broadcast([128, 8, 64]), ALU.mult,
               ["fqk0_%d" % sl, "rh"], ["Qn"])
        Kn, Knk = Kns[sl], "Kn%d" % sl
        tt(sqf[:, 1], f_[:, 1], rh[:, 8:16].unsqueeze(2).to_broadcast([128, 8, 64]), ALU.mult,
           ["fqk1_%d" % sl, "rh", "sqf"], ["sqf"])
        tt(Kn[:, :, 0:64], sqf[:, 1], gk[:].unsqueeze(1).to_broadcast([128, 8, 64]), ALU.mult, ["sqf", "gk"], [Knk])

    def B2s2(it):
        sl = it % 2
        Kn, Knk = Kns[sl], "Kn%d" % sl
        pkt = pv(P_q, [65, 8, 128], BF16, parts=65)
        for h in range(8):
            tr(pkt[:, h, :], Kn[:, h, :], [Knk], ["P_q"])
        cp(KT[:, :, it * 128:(it + 1) * 128], pkt, ["P_q"], ["KT"])

    if stage >= 2:
        NB = 2 * NT
        cur_junk[0] = junkB
        xload(0)
        xload(1)
        FA(0)
        FA(1)
        for g in ("fq", "fk", "fv", "ff"):
            PGB(0, g)
        B2sq(0)
        B1b(0)
        for k in range(NB):
            n = k + 1 if k + 1 < NB else None
            if k + 2 < NB:
                FAa(k + 2)
            B2s1(k)
            if k + 2 < NB:
                FAb(k + 2)
            B2s1b(k)
            if n is not None:
                PGB(n, "fq")
                PGB(n, "fk")
                B2sq(n)
                PGB(n, "fv")
            B2s2(k)
            if n is not None:
                PGB(n, "ff")
                B1b(n)
    if stage >= 2:
        zfl = zf[:].rearrange("p a b -> p (a b)")
        act(zfl, zfl, AF.Exp, ["zf"], ["zf"], scale=-1.0)
        act(zfl, zfl, AF.Ln, ["zf", "oneb"], ["zf"], bias=oneb[:])
        pg1 = P_a[:, 0:256]
        pg2 = P_b[:, 0:256]
        mm(pg1, trif[:], zfl, True, True, ["trif", "zf"], ["P_a"])
        mm(pg2, onesf[:], zfl, True, True, ["onesf", "zf"], ["P_b"])
        Gsl = Gs[:].rearrange("p a b -> p (a b)")
        cp(Gt[:].rearrange("p a b -> p (a b)"), pg2, ["P_b"], ["Gt"])
        cp(Gsl, pg1, ["P_a"], ["Gs"])
        bufs_ = [(Gt, "Gt"), (zf, "zf")]
        cur_ = 0
        for st_ in (1, 2, 4, 8, 16):
            (a_, ak), (b_, bk) = bufs_[cur_], bufs_[1 - cur_]
            tt(b_[:, st_:32, :], a_[:, st_:32, :], a_[:, 0:32 - st_, :], ALU.add, [ak], [bk])
            tt(b_[:, 0:st_, :], a_[:, 0:st_, :], a_[:, 0:st_, :], ALU.max, [ak], [bk])
            cur_ = 1 - cur_
        fin_, fink = bufs_[cur_]
        tt(Gs[:, 1:32, :], Gs[:, 1:32, :], fin_[:, 0:31, :], ALU.add, ["Gs", fink], ["Gs"])
        qn4 = Qn.rearrange("p a (b c) -> p a b c", b=8)
        ts(qn4[:, :, :, 64], Gs[:, 16:32, :], -1.0, None, ALU.mult, ALU.bypass, ["Gs"], ["Qn"])
        ts(Gs[:, 0:16, :], Gs[:, 0:16, :], pb[:], None, ALU.add, ALU.bypass, ["Gs", "pb"], ["Gs"])
    S.flush()

    QT = V(OZ, [65, 8, NTOK], BF16, parts=65)
    for t_ in range(NT if stage >= 2 else 0):
        pqt = pv(P_q if t_ % 2 == 0 else P_t, [65, 8, 128], BF16, parts=65)
        pk_ = "P_q" if t_ % 2 == 0 else "P_t"
        qnv = Qn[:, t_, :].rearrange("p (a b) -> p a b", a=8)
        for h in range(8):
            tr(pqt[:, h, :], qnv[:, h, :], ["Qn"], [pk_])
        cp(QT[:, :, t_ * 128:(t_ + 1) * 128], pqt, [pk_], ["QT"])
    S.flush()

    foxT2 = V(OW, [128, 4, NTOK], BF16)
    ftmp = V(OW + 16 * KB, [128, 512], BF16)
    pTs = [V(OW + 33 * KB + i * KB, [128, 512], BF16) for i in range(3)]
    osb = V(OW + 36 * KB, [128, 512])
    rl = V(OW + 38 * KB, [128, 512])
    wor = V(OW + 17 * KB, [128, 4, D], BF16)
    wof2 = V(OW + 25 * KB, [128, 4, D], BF16)
    if stage >= 4:
        for c in range(4):
            ld(wor[:, c, :], w_out[c * 128:(c + 1) * 128, :], "wor", ["wor%d" % c], eng="gpsimd")
        for p_ in range(4):
            ld(wof2[:, p_, :], w_out[512 + p_ * 128:512 + (p_ + 1) * 128, :], "wof", ["wof%d" % p_], eng="gpsimd")
    if stage >= 3:
        S.op("gpsimd", lambda e: e.memset(rl, 0.0), writes=["rl"])
    psS = [P_a, P_b, P_c]
    psk = ["P_a", "P_b", "P_c"]
    psO = [P_o, P_q]
    pok = ["P_o", "P_q"]
    tiles = []
    for h in range(8 if stage >= 3 else 0):
        for qi in range(4):
            nk = 16 + 4 * (qi + 1)
            for kt in range(nk):
                tiles.append((h, qi, kt, nk))

    def emit_S(idx):
        h, qi, kt, nk = tiles[idx]
        j = kt - (16 + 4 * qi)
        q0 = 128 * j if j > 0 else 0
        ps_, psk_ = psS[idx % 3], psk[idx % 3]
        diag = j >= 0
        mm(ps_[:, q0:512], KT[:, h, kt * 128:(kt + 1) * 128], QT[:, h, qi * 512 + q0:(qi + 1) * 512],
           True, not diag, ["KT", "QT"], [psk_])
        if diag:
            mm(ps_[:, q0:q0 + 128], ident[:], tmb[:], False, True, ["ident", "tmb"], [psk_])

    def make_norm(h, qi, po_, pk_o):
        def fn():
            S.op("vector", lambda e: e.reciprocal(out=rl[64:65, :], in_=po_[64:65, 0:512]), [pk_o], ["rl"])
            mm(P_t[0:64, 0:512], sel64[:], rl, True, True, ["sel64", "rl"], ["P_t"])
            cp(osb[0:64, :], po_[0:64, 0:512], [pk_o], ["osb"])
            cols = slice(qi * 512, (qi + 1) * 512)
            if h % 2 == 0:
                tt(foxT2[0:64, h // 2, cols], osb[0:64, :], P_t[0:64, 0:512], ALU.mult, ["osb", "P_t"], ["foxT"])
            else:
                tt(ftmp[0:64, :], osb[0:64, :], P_t[0:64, 0:512], ALU.mult, ["osb", "P_t"], ["ftmp"])
                mm(P_t[:, 0:512], shiftb[0:64, :], ftmp[0:64, :], True, True, ["shiftb", "ftmp"], ["P_t"])
                cp(foxT2[64:128, h // 2, cols], P_t[64:128, 0:512], ["P_t"], ["foxT"])
        return fn

    LOOK = 2
    for i0 in range(min(LOOK, len(tiles))):
        emit_S(i0)
    pending = None
    since = 0
    for idx, (h, qi, kt, nk) in enumerate(tiles):
        if idx + LOOK < len(tiles):
            emit_S(idx + LOOK)
        g = h * 4 + qi
        po_, pk_o = psO[g % 2], pok[g % 2]
        j = kt - (16 + 4 * qi)
        q0 = 128 * j if j > 0 else 0
        ps_, psk_ = psS[idx % 3], psk[idx % 3]
        pT, pTk = pTs[idx % 3], "pT%d" % (idx % 3)
        act(pT[:, q0:512], ps_[:, q0:512], AF.Exp, [psk_, "Gs"], [pTk], bias=Gs[:, kt, h:h + 1])
        vav = VA[:, kt, :].rearrange("p (a b) -> p a b", a=8)
        mm(po_[0:65, q0:512], vav[:, h, :], pT[:, q0:512], kt == 0, kt == nk - 1, ["VA", pTk], [pk_o])
        since += 1
        if pending is not None and since >= 6:
            pending()
            pending = None
        if kt == nk - 1:
            if pending is not None:
                pending()
            pending = make_norm(h, qi, po_, pk_o)
            since = 0
    if pending is not None:
        pending()
    S.flush()

    if debug and stage == 3:
        dbt = V(OZ, [128, 4096])
        S.op("vector", lambda e: e.memset(dbt, 0.0), ["QT"], ["dbt"])
        cp(dbt[0:64, 0:2048], foxT2[0:64, 0, :], ["foxT"], ["dbt"])
        cp(dbt[64:128, 2048:4096], foxT2[64:128, 3, :], ["foxT"], ["dbt"])
        ld(dbg, dbt, "dbg", [], reads=["dbt"])

    hres = V(OX, [128, NT, D])
    wq = V(OY, [128, 8, D], BF16)
    wo = V(OY + 16 * KB, [128, 8, D], BF16)
    if stage >= 5:
        for c in range(8):
            ld(wq[:, c, :], w_xq[c * 128:(c + 1) * 128, :], "wq", ["wq%d" % c], eng="gpsimd")
        for c in range(8):
            ld(wo[:, c, :], w_xo[c * 128:(c + 1) * 128, :], "wo", ["wo%d" % c], eng="gpsimd")
    x3 = [V(OZ + 0, [128, 1024]), V(OZ + 4 * KB, [128, 1024])]
    for t_ in range(NT if stage >= 4 else 0):
        sl = t_ % 2
        ld(x3[sl], x_own[t_ * 128:(t_ + 1) * 128, :], "xt%d" % sl, ["x3_%d" % sl])
        for half in range(2):
            ps_, psk_ = (P_a, "P_a") if half == 0 else (P_b, "P_b")
            for c in range(4):
                mm(ps_[:, 0:512], mixT[:, c, t_ * 128:(t_ + 1) * 128], wor[:, c, half * 512:(half + 1) * 512],
                   c == 0, False, ["mixT", "wor3"], [psk_])
            for p_ in range(4):
                mm(ps_[:, 0:512], foxT2[:, p_, t_ * 128:(t_ + 1) * 128], wof2[:, p_, half * 512:(half + 1) * 512],
                   False, p_ == 3, ["foxT", "wof3"], [psk_])
            tt(hres[:, t_, half * 512:(half + 1) * 512], x3[sl][:, half * 512:(half + 1) * 512], ps_[:, 0:512], ALU.add,
               ["x3_%d" % sl, psk_], ["h%d" % t_])
    S.flush()

    wq = V(OY, [128, 8, D], BF16)
    wo = V(OY + 16 * KB, [128, 8, D], BF16)
    wkv = V(OW, [128, 8, 2 * D], BF16)
    kTx = V(OM, [128, 8, 256], BF16)
    vbx = V(OM + 4 * KB, [128, 2, D], BF16)
    kfx = V(OM + 8 * KB, [128, D])
    ksq = V(OM + 12 * KB, [128, D])
    mt4 = [V(OZ + 0, [128, 1024]), V(OZ + 4 * KB, [128, 1024])]
    xb4 = V(OZ + 8 * KB, [128, 1024], BF16)
    hT4 = V(OZ + 10 * KB, [128, 8, 512], BF16)
    mT4 = V(OZ + 18 * KB, [128, 8, 256], BF16)
    kbx = V(OZ + 22 * KB, [128, D], BF16)
    qT4 = V(OW + 0, [128, 8, 512], BF16)
    sq4 = V(OW + 8 * KB, [128, 8, 512], BF16)
    rq4 = V(OW + 16 * KB, [128, 512])
    pT4 = [V(OW + 18 * KB, [128, 2, 512], BF16), V(OW + 20 * KB, [128, 2, 512], BF16)]
    oT4 = V(OW + 22 * KB, [128, 8, 512], BF16)
    rl4 = V(OW + 30 * KB, [128, 512])
    ou4 = V(OW + 32 * KB, [128, 512])
    rs4 = sb("rs4", [128, 8])
    if stage >= 5:
        for c in range(8):
            ld(wkv[:, c, 0:D], w_xkv[c * 128:(c + 1) * 128, 0:D], "wkvk", ["wkvK%d" % c], eng="gpsimd")
        for c in range(8):
            ld(wkv[:, c, D:2 * D], w_xkv[c * 128:(c + 1) * 128, D:2 * D], "wkvv", ["wkvV%d" % c], eng="gpsimd")
        for mtile in range(2):
            ld(mt4[mtile], mem[mtile * 128:(mtile + 1) * 128, :], "xt%d" % mtile, ["mt%d" % mtile])
            front(mt4[mtile], "mt%d" % mtile, GME, mT4[:, :, mtile * 128:(mtile + 1) * 128], "mT4", xb4, "xb4",
                  rs4[:, 0:1], "rs4a", True)
            for half in range(2):
                for c in range(8):
                    mm(P_a[:, 0:512], mT4[:, c, mtile * 128:(mtile + 1) * 128], wkv[:, c, half * 512:(half + 1) * 512],
                       c == 0, c == 7, ["mT4", "wkvK7"], ["P_a"])
                act(kfx[:, half * 512:(half + 1) * 512], P_a[:, 0:512], AF.Copy, ["P_a"], ["kfx"])
            for half in range(2):
                for c in range(8):
                    mm(P_b[:, 0:512], mT4[:, c, mtile * 128:(mtile + 1) * 128],
                       wkv[:, c, D + half * 512:D + (half + 1) * 512], c == 0, c == 7, ["mT4", "wkvV7"], ["P_b"])
                act(vbx[:, mtile, half * 512:(half + 1) * 512], P_b[:, 0:512], AF.Copy, ["P_b"], ["vbx"])
            tt(ksq, kfx, kfx, ALU.mult, ["kfx"], ["ksq"])
            red(rs4[:, 4:8], ksq.rearrange("p (a b) -> p a b", a=4), ["ksq"], ["rs4k"])
            act(rs4[:, 4:8], rs4[:, 4:8], AF.Ln, ["rs4k", "epsb"], ["rs4k"], scale=1.0 / 256, bias=epsb[:])
            act(rs4[:, 4:8], rs4[:, 4:8], AF.Exp, ["rs4k"], ["rs4k"], scale=-0.5)
            tt(ksq.rearrange("p (a b) -> p a b", a=4), kfx.rearrange("p (a b) -> p a b", a=4),
               rs4[:, 4:8].unsqueeze(2).to_broadcast([128, 4, 256]), ALU.mult, ["kfx", "rs4k", "ksq"], ["ksq"])
            tt(kbx.rearrange("p (a b) -> p a b", a=4), ksq.rearrange("p (a b) -> p a b", a=4),
               gxk[:].unsqueeze(1).to_broadcast([128, 4, 256]), ALU.mult, ["ksq", "gxk"], ["kbx"])
            pkx = pv(P_q, [128, 8, 128], BF16)
            for c in range(8):
                tr(pkx[:, c, :], kbx[:, c * 128:(c + 1) * 128], ["kbx"], ["P_q"])
            cp(kTx[:, :, mtile * 128:(mtile + 1) * 128], pkx, ["P_q"], ["kTx"])
    S.flush()
    hT4s = [V(OZ + 0, [128, 8, 512], BF16), V(OZ + 10 * KB, [128, 8, 512], BF16)]
    xb4s = [V(OZ + 8 * KB, [128, 1024], BF16), V(OZ + 18 * KB, [128, 1024], BF16)]
    junk4 = V(OZ + 20 * KB, [128, 1024], BF16)
    rq4s = [V(OZ + 24 * KB + i * 2 * KB, [128, 512]) for i in range(4)]
    pT4s = [V(OW + 16 * KB + i * 2 * KB, [128, 2, 512], BF16) for i in range(4)]
    oT4 = V(OW + 24 * KB, [128, 8, 512], BF16)
    rl4s = [V(OW + 32 * KB + i * 2 * KB, [128, 512]) for i in range(2)]

    def F4(b):
        for ti in range(4):
            t_ = b * 4 + ti
            k_ = ti % 2
            front(hres[:, t_, :], "h%d" % t_, GX, hT4s[b % 2][:, :, ti * 128:(ti + 1) * 128], "hT4_%d" % (b % 2),
                  xb4s[k_], "xb4_%d" % k_, rs4[:, 1 + k_:2 + k_], "rs4b%d" % k_, True, junk=junk4)

    def Q4(b):
        hT, hk = hT4s[b % 2], "hT4_%d" % (b % 2)
        for c2 in range(8):
            ps_, psk_ = (P_a, "P_a") if c2 % 2 == 0 else (P_b, "P_b")
            for c in range(8):
                mm(ps_[:, 0:512], wq[:, c, c2 * 128:(c2 + 1) * 128], hT[:, c, :], c == 0, c == 7, ["wq7", hk], [psk_])
            cp(qT4[:, c2, :], ps_[:, 0:512], [psk_], ["qT4_%d" % c2])
            tt(sq4[:, c2, :], qT4[:, c2, :], qT4[:, c2, :], ALU.mult, ["qT4_%d" % c2], ["sq4_%d" % c2], eng="gpsimd")

    def H4(b):
        for hd in range(4):
            mm(P_c[:, 0:512], onesb[:], sq4[:, 2 * hd, :], True, False, ["onesb", "sq4_%d" % (2 * hd)], ["P_c"])
            mm(P_c[:, 0:512], onesb[:], sq4[:, 2 * hd + 1, :], False, True, ["onesb", "sq4_%d" % (2 * hd + 1)], ["P_c"])
            rk = "rq4_%d" % hd
            act(rq4s[hd], P_c[:, 0:512], AF.Ln, ["P_c", "epsb"], [rk], scale=1.0 / 256, bias=epsb[:])
            act(rq4s[hd], rq4s[hd], AF.Exp, [rk], [rk], scale=-0.5)
            for half in range(2):
                c2 = 2 * hd + half
                tt(qT4[:, c2, :], qT4[:, c2, :], rq4s[hd], ALU.mult, ["qT4_%d" % c2, rk], ["qT4_%d" % c2])
        for hd in range(4):
            pT_, pTk = pT4s[hd], "pT4_%d" % hd
            for mtile in range(2):
                ps_, psk_ = (P_a, "P_a") if mtile == 0 else (P_b, "P_b")
                for half in range(2):
                    c2 = 2 * hd + half
                    mm(ps_[:, 0:512], kTx[:, c2, mtile * 128:(mtile + 1) * 128], qT4[:, c2, :], half == 0, half == 1,
                       ["kTx", "qT4_%d" % c2], [psk_])
                act(pT_[:, mtile, :], ps_[:, 0:512], AF.Exp, [psk_], [pTk])
        for hd in range(4):
            pT_, pTk = pT4s[hd], "pT4_%d" % hd
            rl_, rlk = rl4s[hd % 2], "rl4_%d" % (hd % 2)
            mm(P_c[:, 0:512], onesb[:], pT_[:, 0, :], True, False, ["onesb", pTk], ["P_c"])
            mm(P_c[:, 0:512], onesb[:], pT_[:, 1, :], False, True, ["onesb", pTk], ["P_c"])
            act(rl_, P_c[:, 0:512], AF.Ln, ["P_c"], [rlk])
            act(rl_, rl_, AF.Exp, [rlk], [rlk], scale=-1.0)
            for half in range(2):
                c2 = 2 * hd + half
                po_, pk_o = (P_o, "P_o") if half == 0 else (P_q, "P_q")
                for mtile in range(2):
                    mm(po_[:, 0:512], vbx[:, mtile, c2 * 128:(c2 + 1) * 128], pT_[:, mtile, :], mtile == 0, mtile == 1,
                       ["vbx", pTk], [pk_o])
                tt(oT4[:, c2, :], po_[:, 0:512], rl_, ALU.mult, [pk_o, rlk], ["oT4"])

    def O4(b):
        for ti in range(4):
            t_ = b * 4 + ti
            for half in range(2):
                ps_, psk_ = (P_a, "P_a") if half == 0 else (P_b, "P_b")
                for c in range(8):
                    mm(ps_[:, 0:512], oT4[:, c, ti * 128:(ti + 1) * 128], wo[:, c, half * 512:(half + 1) * 512],
                       c == 0, c == 7, ["oT4", "wo7"], [psk_])
                tt(hres[:, t_, half * 512:(half + 1) * 512], hres[:, t_, half * 512:(half + 1) * 512], ps_[:, 0:512],
                   ALU.add, ["h%d" % t_, psk_], ["h%d" % t_])

    if stage >= 5:
        F4(0)
        Q4(0)
        for b_ in range(4):
            if b_ + 1 < 4:
                F4(b_ + 1)
            H4(b_)
            if b_ + 1 < 4:
                Q4(b_ + 1)
            O4(b_)
    S.flush()

    hn5 = V(OZ, [128, 8, NTOK], BF16)
    R0 = OY
    PARTS = [6, 6, 5, 5]

    def wslot(s_):
        base = R0 + s_ * 36 * KB
        return (V(base, [128, 8, 768], BF16), V(base + 12 * KB, [128, 8, 768], BF16),
                V(base + 24 * KB, [128, 6, D], BF16))

    aT = V(R0 + 72 * KB, [128, 6, 512], BF16)
    sg5 = V(R0 + 78 * KB, [128, 512])
    xb5 = V(R0 + 80 * KB, [128, 1024], BF16)
    rs5 = sb("rs5", [128, 1])
    xb5s = [xb5, V(R0 + 82 * KB, [128, 1024], BF16)]
    junk5 = V(R0 + 84 * KB, [128, 1024], BF16)
    rs5b = sb("rs5b", [128, 2])

    last_key = {}

    def load_part(pi):
        nf = PARTS[pi]
        f0 = sum(PARTS[:pi])
        wg_, wu_, wd_ = wslot(pi % 2)
        sk = "ws%d" % (pi % 2)
        i_ = [0]

        def one(dst, src):
            keys = ["%s_%d_%d" % (sk, pi, i_[0])]
            if i_[0] == 0 and pi >= 2:
                keys.append(last_key[pi - 2])
            ld(dst, src, sk, keys, eng="gpsimd")
            i_[0] += 1
            return keys[0]
        last = None
        for c in range(8):
            one(wg_[:, c, 0:nf * 128], w_gate[c * 128:(c + 1) * 128, f0 * 128:(f0 + nf) * 128])
            one(wu_[:, c, 0:nf * 128], w_up[c * 128:(c + 1) * 128, f0 * 128:(f0 + nf) * 128])
        for f in range(nf):
            last = one(wd_[:, f, :], w_down[(f0 + f) * 128:(f0 + f + 1) * 128, :])
        last_key[pi] = last
        return [last]

    def ffn_fronts(blk):
        for t_ in range(blk * 4, blk * 4 + 4):
            k_ = t_ % 2
            front(hres[:, t_, :], "h%d" % t_, GF, hn5[:, :, t_ * 128:(t_ + 1) * 128], "hn5_%d" % (t_ // 4),
                  xb5s[k_], "xb5_%d" % k_, rs5b[:, k_:k_ + 1], "rs5_%d" % k_, True, junk=junk5)

    def ffn_part(pi, wkeys):
        nf = PARTS[pi]
        wg_, wu_, wd_ = wslot(pi % 2)
        for blk in range(4):
            if pi == 0 and blk + 1 < 4:
                ffn_fronts(blk + 1)
            hk = "hn5_%d" % blk
            for f in range(nf):
                for c in range(8):
                    mm(P_a[:, 0:512], wg_[:, c, f * 128:(f + 1) * 128], hn5[:, c, blk * 512:(blk + 1) * 512],
                       c == 0, c == 7, wkeys + [hk], ["P_a"])
                for c in range(8):
                    mm(P_b[:, 0:512], wu_[:, c, f * 128:(f + 1) * 128], hn5[:, c, blk * 512:(blk + 1) * 512],
                       c == 0, c == 7, wkeys + [hk], ["P_b"])
                act(sg5, P_a[:, 0:512], AF.Silu, ["P_a"], ["sg5"])
                tt(aT[:, f, :], sg5, P_b[:, 0:512], ALU.mult, ["sg5", "P_b"], ["aT%d" % f])
            for ti in range(4):
                t_ = blk * 4 + ti
                for half in range(2):
                    ps_, psk_ = (P_c, "P_c") if half == 0 else (P_o, "P_o")
                    for f in range(nf):
                        mm(ps_[:, 0:512], aT[:, f, ti * 128:(ti + 1) * 128], wd_[:, f, half * 512:(half + 1) * 512],
                           f == 0, f == nf - 1, ["aT%d" % f] + wkeys, [psk_])
                    tt(hres[:, t_, half * 512:(half + 1) * 512], hres[:, t_, half * 512:(half + 1) * 512],
                       ps_[:, 0:512], ALU.add, ["h%d" % t_, psk_], ["h%d" % t_])

    if stage >= 6:
        wk0 = load_part(0)
        wk1 = load_part(1)
        ffn_fronts(0)
        ffn_part(0, wk0)
        wk2 = load_part(2)
        ffn_part(1, wk1)
        wk3 = load_part(3)
        ffn_part(2, wk2)
        ffn_part(3, wk3)
    if stage >= 4:
        for t_ in range(NT):
            ld(y[t_ * 128:(t_ + 1) * 128, :], hres[:, t_, :], "yout", [], reads=["h%d" % t_])
    S.flush(final=True)
    return nc


def _consts(s):
    H = 8
    idx = np.arange(128, dtype=np.float32)
    log_g = np.log(1.0 - 2.0 ** (-5.0 - np.arange(H, dtype=np.float32))).astype(np.float32)
    inv_freq = (10000.0 ** (-np.arange(0, 64, 2, dtype=np.float32) / 64)).astype(np.float32)
    pos_own = np.arange(NTOK, dtype=np.float32) + s * NTOK
    pos_pre = np.arange(NTOK, dtype=np.float32)
    pos = np.concatenate([pos_pre, pos_own]).astype(np.float32)
    ang = (pos[:, None] * inv_freq[None, :]).astype(np.float32)
    cs = np.concatenate([np.cos(ang), np.sin(ang)], axis=1).astype(np.float32)
    cst = cs.reshape(32, 128, 64).transpose(1, 0, 2).reshape(128, 32 * 64)
    i_ = idx[None, :]
    j_ = idx[:, None]
    same = (i_ // 64) == (j_ // 64)
    lower = (i_ >= 64) & (j_ < 64)
    d2 = np.zeros((128, H, 128), np.float32)
    for h in range(H):
        a = np.exp(log_g[h] * np.abs(i_ - j_)).astype(np.float32)
        b = np.exp(log_g[h] * (i_ - j_)).astype(np.float32)
        d2[:, h, :] = np.where(same, a, np.where(lower, b, 0.0)) * 0.125
    qd = np.zeros((128, 4, 128), np.float32)
    cd = np.zeros((128, 4, 64), np.float32)
    for p in range(128):
        for pr in range(4):
            h = 2 * pr + p // 64
            qd[p, pr, :] = np.exp(log_g[h] * (idx + 1.0)) * 0.125
            cd[p, pr, :] = np.exp(log_g[h] * 128.0)
    kd = np.exp(log_g[None, :] * (127.0 - idx[:, None])).astype(np.float32)
    me = np.zeros((128, 4, 128), np.float32); me[0:64] = 1.0
    mo = np.zeros((128, 4, 128), np.float32); mo[64:128] = 1.0
    rt4 = np.concatenate([me.reshape(128, -1), mo.reshape(128, -1), (qd * me).reshape(128, -1),
                          (qd * mo).reshape(128, -1)], axis=1).astype(np.float32)
    return dict(rt4=np.ascontiguousarray(rt4), cst=np.ascontiguousarray(cst), d2t=d2.reshape(128, -1), qdt=qd.reshape(128, -1),
                cdt=cd.reshape(128, -1), kdt=kd)


_NC_CACHE = {}


def kernel(x, mem, g_mix, w_in, b_forget, g_ret_out, g_fox_q, g_fox_k, w_out, g_xattn, w_xq, w_xkv, g_mem,
           g_xq, g_xk, w_xo, g_ffn, w_gate, w_up, w_down, _stage=99, _debug=False):
    f = lambda a: np.ascontiguousarray(np.asarray(a, dtype=np.float32))
    x, mem = f(x), f(mem)
    key = (_stage, _debug)
    if key not in _NC_CACHE:
        _NC_CACHE[key] = build(_stage, _debug)
    nc = _NC_CACHE[key]
    gcol = lambda g: f(g).reshape(8, 128).T
    gvec = np.ascontiguousarray(np.concatenate([gcol(g_mix[0]), gcol(g_xattn[0]), gcol(g_mem[0]), gcol(g_ffn[0])], axis=1))
    rowc = np.concatenate([f(g_ret_out[0]).reshape(-1), f(g_fox_q[0]), f(g_fox_k[0]), f(b_forget[0]),
                           f(g_xq[0]), f(g_xk[0])])[None, :]
    shared = dict(w_in=f(w_in[0]), w_out=f(w_out[0]), w_xq=f(w_xq[0]), w_xkv=f(w_xkv[0]), w_xo=f(w_xo[0]),
                  w_gate=f(w_gate[0]), w_up=f(w_up[0]), w_down=f(w_down[0]), gvec=gvec, rowc=f(rowc))
    cs = [_consts(0), _consts(1)]
    zeros = np.zeros((NTOK, D), np.float32)
    in_maps = []
    for core in range(8):
        b, s = core // 2, core % 2
        m = dict(shared)
        m.update(cs[s])
        m["x_own"] = np.ascontiguousarray(x[b, s * NTOK:(s + 1) * NTOK])
        m["x_pre"] = np.ascontiguousarray(x[b, 0:NTOK]) if s == 1 else zeros
        m["mem"] = mem[b]
        m["pbv"] = np.full((128, 1), 0.0 if s == 1 else NEG, np.float32)
        in_maps.append(m)
    res = run_bass_kernel_spmd(nc, in_maps, core_ids=list(range(8)))
    out = np.empty((4, 4096, D), np.float32)
    for core in range(8):
        b, s = core // 2, core % 2
        out[b, s * NTOK:(s + 1) * NTOK] = res.results[core]["y"]
    if _debug:
        return out, [r["dbg"] for r in res.results]
    return out
```
